# Optimizing a Trainium2 kernel written in Bass

```python
import jax, jax.numpy as jnp
from jax import lax
import numpy as np

D_MODEL = 1024
BATCH = 8
SEQ = 2048
DEPTH = 1

GRID_W = 64
CTX_LEN = 256
Q_BLOCK = 128
ROPE_THETA = 10000.0
EPS = 1e-6

A_HEADS = 8
A_KV_HEADS = 2
A_HEAD_DIM = 64
B_HEADS = 8
B_Q_RANK = 384
B_KV_RANK = 256
B_NOPE_DIM = 64
B_ROPE_DIM = 32
B_V_DIM = 64
FFN_HIDDEN = ((8 * D_MODEL // 3 + 255) // 256) * 256

A_SCALE = A_HEAD_DIM ** -0.5
B_SCALE = (B_NOPE_DIM + B_ROPE_DIM) ** -0.5
DEEPNORM_ALPHA = (2.0 * DEPTH) ** 0.25
DEEPNORM_BETA = (8.0 * DEPTH) ** -0.25

IN_SPLIT_SIZES = (
    A_HEADS * A_HEAD_DIM,
    A_KV_HEADS * A_HEAD_DIM,
    A_KV_HEADS * A_HEAD_DIM,
    B_Q_RANK,
    B_KV_RANK,
    B_ROPE_DIM,
    D_MODEL,
    D_MODEL,
)
W_IN_COLS = sum(IN_SPLIT_SIZES)
IN_SPLIT_IDX = tuple(sum(IN_SPLIT_SIZES[:i + 1]) for i in range(len(IN_SPLIT_SIZES) - 1))

kernel_name = "hybrid_gqa_mla_deepnorm_dit_layer"


def layer_norm(x):
    xf = x.astype(jnp.float32)
    mu = jnp.mean(xf, axis=-1, keepdims=True)
    var = jnp.mean(jnp.square(xf - mu), axis=-1, keepdims=True)
    return ((xf - mu) * lax.rsqrt(var + EPS)).astype(x.dtype)


def rms_norm(x, g):
    xf = x.astype(jnp.float32)
    y = xf * lax.rsqrt(jnp.mean(jnp.square(xf), axis=-1, keepdims=True) + EPS)
    return y.astype(x.dtype) * g


def modulate(x, shift, scale):
    return x * (1 + scale) + shift


def post_norm(x, y, g, b):
    return layer_norm(DEEPNORM_ALPHA * x + y) * g + b


def rope_1d(x, pos):
    half = x.shape[-1] // 2
    freqs = ROPE_THETA ** (-jnp.arange(half, dtype=jnp.float32) / half)
    ang = pos.astype(jnp.float32)[:, None] * freqs[None, :]
    cos = jnp.cos(ang)[None, :, None, :].astype(x.dtype)
    sin = jnp.sin(ang)[None, :, None, :].astype(x.dtype)
    x1, x2 = x[..., :half], x[..., half:]
    return jnp.concatenate([x1 * cos - x2 * sin, x1 * sin + x2 * cos], axis=-1)


def axial_rope(x, rows, cols):
    d = x.shape[-1] // 2
    return jnp.concatenate([rope_1d(x[..., :d], rows), rope_1d(x[..., d:], cols)], axis=-1)


def project_streams(h, w_in, q_norm_a, k_norm_a, cq_norm, ckv_norm, w_uq, w_ukv, rows, cols):
    b, t, _ = h.shape
    p = h @ w_in
    q_a, k_a, v_a, c_q, c_kv, k_r, g_a, g_b = jnp.split(p, IN_SPLIT_IDX, axis=-1)
    q_a = rms_norm(q_a.reshape(b, t, A_HEADS, A_HEAD_DIM), q_norm_a)
    k_a = rms_norm(k_a.reshape(b, t, A_KV_HEADS, A_HEAD_DIM), k_norm_a)
    v_a = v_a.reshape(b, t, A_KV_HEADS, A_HEAD_DIM)
    q_b = (rms_norm(c_q, cq_norm) @ w_uq).reshape(b, t, B_HEADS, B_NOPE_DIM + B_ROPE_DIM)
    q_nope, q_rope = q_b[..., :B_NOPE_DIM], q_b[..., B_NOPE_DIM:]
    kv = (rms_norm(c_kv, ckv_norm) @ w_ukv).reshape(b, t, B_HEADS, B_NOPE_DIM + B_V_DIM)
    k_nope, v_b = kv[..., :B_NOPE_DIM], kv[..., B_NOPE_DIM:]
    k_r = k_r[:, :, None, :]
    if rows is not None:
        q_a = axial_rope(q_a, rows, cols)
        k_a = axial_rope(k_a, rows, cols)
        q_rope = axial_rope(q_rope, rows, cols)
        k_r = axial_rope(k_r, rows, cols)
    q_b = jnp.concatenate([q_nope, q_rope], axis=-1)
    k_b = jnp.concatenate([k_nope, jnp.broadcast_to(k_r, (b, t, B_HEADS, B_ROPE_DIM))], axis=-1)
    return q_a, k_a, v_a, q_b, k_b, v_b, g_a, g_b


def block_attention(q, k, v, scale):
    b, s, h, dq = q.shape
    hkv, dv = k.shape[2], v.shape[-1]
    g = h // hkv
    nblk = s // Q_BLOCK
    qb = q.reshape(b, nblk, Q_BLOCK, hkv, g, dq).transpose(1, 0, 2, 3, 4, 5)

    def one_block(q_blk):
        sc = jnp.einsum('bqkgd,btkd->bkgqt', q_blk, k).astype(jnp.float32) * scale
        p = jax.nn.softmax(sc, axis=-1).astype(v.dtype)
        return jnp.einsum('bkgqt,btkd->bqkgd', p, v)

    out = lax.map(one_block, qb)
    return out.transpose(1, 0, 2, 3, 4, 5).reshape(b, s, h * dv)


def merge_branches(o_a, o_b, g_a, g_b, w_proj_a, w_proj_b, w_out):
    y = jax.nn.sigmoid(g_a) * (o_a @ w_proj_a) + jax.nn.sigmoid(g_b) * (o_b @ w_proj_b)
    return y @ w_out


def swiglu(h, w_up, w_down):
    a, u = jnp.split(h @ w_up, 2, axis=-1)
    return (jax.nn.silu(a) * u) @ w_down


def setup_inputs(seed: int = 0) -> dict:
    key = jax.random.key(seed)
    ks = jax.random.split(key, 24)
    L, D, F = DEPTH, D_MODEL, FFN_HIDDEN
    nrm = lambda k, shape, s: jax.random.normal(k, shape, jnp.float32) * s
    return {
        "x": nrm(ks[0], (BATCH, SEQ, D), 1.0),
        "c": nrm(ks[1], (BATCH, D), 1.0),
        "ctx": nrm(ks[2], (BATCH, CTX_LEN, D), 1.0),
        "c_ctx": nrm(ks[3], (D,), 1.0),
        "w_mod": nrm(ks[4], (L, D, 6 * D), D ** -0.5),
        "b_mod": nrm(ks[5], (L, 6 * D), 0.02),
        "w_in": nrm(ks[6], (L, D, W_IN_COLS), D ** -0.5),
        "q_norm_a": 1.0 + nrm(ks[7], (L, A_HEAD_DIM), 0.02),
        "k_norm_a": 1.0 + nrm(ks[8], (L, A_HEAD_DIM), 0.02),
        "cq_norm": 1.0 + nrm(ks[9], (L, B_Q_RANK), 0.02),
        "ckv_norm": 1.0 + nrm(ks[10], (L, B_KV_RANK), 0.02),
        "w_uq": nrm(ks[11], (L, B_Q_RANK, B_HEADS * (B_NOPE_DIM + B_ROPE_DIM)), B_Q_RANK ** -0.5),
        "w_ukv": nrm(ks[12], (L, B_KV_RANK, B_HEADS * (B_NOPE_DIM + B_V_DIM)), B_KV_RANK ** -0.5),
        "w_proj_a": nrm(ks[13], (L, A_HEADS * A_HEAD_DIM, D), (A_HEADS * A_HEAD_DIM) ** -0.5),
        "w_proj_b": nrm(ks[14], (L, B_HEADS * B_V_DIM, D), (B_HEADS * B_V_DIM) ** -0.5),
        "w_out": nrm(ks[15], (L, D, D), DEEPNORM_BETA * D ** -0.5),
        "ln1_g": 1.0 + nrm(ks[16], (L, D), 0.02),
        "ln1_b": nrm(ks[17], (L, D), 0.02),
        "w_up": nrm(ks[18], (L, D, 2 * F), D ** -0.5),
        "w_down": nrm(ks[19], (L, F, D), DEEPNORM_BETA * F ** -0.5),
        "ln2_g": 1.0 + nrm(ks[20], (L, D), 0.02),
        "ln2_b": nrm(ks[21], (L, D), 0.02),
    }


def reference(x, c, ctx, c_ctx, w_mod, b_mod, w_in, q_norm_a, k_norm_a, cq_norm, ckv_norm,
              w_uq, w_ukv, w_proj_a, w_proj_b, w_out, ln1_g, ln1_b, w_up, w_down, ln2_g, ln2_b):
    s = x.shape[1]
    n_rows = s // GRID_W
    rows = jnp.repeat(jnp.arange(n_rows, dtype=jnp.int32), GRID_W)
    cols = jnp.tile(jnp.arange(GRID_W, dtype=jnp.int32), n_rows)

    for l in range(DEPTH):
        mod = jax.nn.silu(c) @ w_mod[l] + b_mod[l]
        sh1, sc1, gt1, sh2, sc2, gt2 = [m[:, None, :] for m in jnp.split(mod, 6, axis=-1)]
        mod_c = jax.nn.silu(c_ctx) @ w_mod[l] + b_mod[l]
        csh1, csc1, cgt1, csh2, csc2, cgt2 = jnp.split(mod_c, 6, axis=-1)

        h = modulate(layer_norm(x), sh1, sc1)
        hc = modulate(layer_norm(ctx), csh1, csc1)
        qa, ka, va, qb, kb, vb, ga, gb = project_streams(
            h, w_in[l], q_norm_a[l], k_norm_a[l], cq_norm[l], ckv_norm[l], w_uq[l], w_ukv[l], rows, cols)
        qac, kac, vac, qbc, kbc, vbc, gac, gbc = project_streams(
            hc, w_in[l], q_norm_a[l], k_norm_a[l], cq_norm[l], ckv_norm[l], w_uq[l], w_ukv[l], None, None)

        oa = block_attention(qa, jnp.concatenate([kac, ka], axis=1),
                             jnp.concatenate([vac, va], axis=1), A_SCALE)
        ob = block_attention(qb, jnp.concatenate([kbc, kb], axis=1),
                             jnp.concatenate([vbc, vb], axis=1), B_SCALE)
        y = merge_branches(oa, ob, ga, gb, w_proj_a[l], w_proj_b[l], w_out[l])

        if l < DEPTH - 1:
            oac = block_attention(qac, kac, vac, A_SCALE)
            obc = block_attention(qbc, kbc, vbc, B_SCALE)
            yc = merge_branches(oac, obc, gac, gbc, w_proj_a[l], w_proj_b[l], w_out[l])
            ctx = post_norm(ctx, cgt1 * yc, ln1_g[l], ln1_b[l])
            hc2 = modulate(layer_norm(ctx), csh2, csc2)
            ctx = post_norm(ctx, cgt2 * swiglu(hc2, w_up[l], w_down[l]), ln2_g[l], ln2_b[l])

        x = post_norm(x, gt1 * y, ln1_g[l], ln1_b[l])

        h2 = modulate(layer_norm(x), sh2, sc2)
        x = post_norm(x, gt2 * swiglu(h2, w_up[l], w_down[l]), ln2_g[l], ln2_b[l])
    return x
```

```python
import numpy as np
import concourse.bass as bass
import concourse.mybir as mybir
from concourse.bass_utils import run_bass_kernel_spmd

F32 = mybir.dt.float32
BF16 = mybir.dt.bfloat16
AF = mybir.ActivationFunctionType
ALU = mybir.AluOpType

D = 1024
S = 2048
CT = 256
T = S + CT
NKT = T // 128
FH = 2816
NF = FH // 128
EPS = 1e-6
ALPHA = 2.0 ** 0.25
A_SCALE = 64 ** -0.5
B_SCALE = 96 ** -0.5
DEBUG = False
import os as _os
OPT_LN = True
OPT_RF = False

CH_Q, CH_QR, CH_K, CH_KR, CH_V, CH_CQ, CH_CKV, CH_KRR, CH_GA, CH_GB = 0, 4, 8, 10, 12, 13, 16, 18, 19, 27
N_WIN_CH = 35


class Buf:
    __slots__ = ("name", "w", "r", "psum")

    def __init__(self, name, psum=False):
        self.name = name
        self.w = None
        self.r = []
        self.psum = psum


class DmaSem:
    __slots__ = ("sem", "count", "last", "q")

    def __init__(self, sem=None):
        self.sem = sem
        self.count = 0
        self.last = None
        self.q = None


class Op:
    __slots__ = ("eng", "fn", "deps", "sig", "sem", "val", "dma")

    def __init__(self, eng, fn, dma):
        self.eng = eng
        self.fn = fn
        self.deps = []
        self.sig = False
        self.sem = None
        self.val = 0
        self.dma = dma


class Sched:
    ENGS = ("pe", "act", "dve", "pool", "sp")

    def __init__(self):
        self.ops = {e: [] for e in self.ENGS}

    def op(self, eng, fn, reads=(), writes=(), dma=None):
        o = Op(eng, fn, dma)
        deps = []
        for b in reads:
            if b.w is not None:
                deps.append(b.w)
            if b.psum:
                deps.extend(r_ for r_ in b.r if r_.eng != eng)
        for b in writes:
            if b.w is not None:
                deps.append(b.w)
            deps.extend(b.r)
        if dma is not None:
            if dma.last is not None:
                deps.append(dma.last)
            dma.count += 1
            dma.last = o
            o.sem = dma.sem
            o.val = 16 * dma.count
            o.sig = True
        seen = set()
        for d in deps:
            if d is o or id(d) in seen:
                continue
            seen.add(id(d))
            if d.dma is None and d.eng == "pe" and eng == "pe":
                continue
            o.deps.append(d)
            d.sig = True
        for b in reads:
            b.r.append(o)
        for b in writes:
            b.w = o
            b.r = []
        self.ops[eng].append(o)
        return o

    def barrier(self, extra=()):
        marks = list(extra)
        for e in ("pe", "act", "dve"):
            for o in reversed(self.ops[e]):
                if o.fn is not None and o.dma is None:
                    marks.append(o)
                    break
        for e in self.ENGS:
            o = Op(e, None, None)
            for m in marks:
                o.deps.append(m)
                m.sig = True
            self.ops[e].append(o)


def build_program():
    nc = bass.Bass("TRN2", target_bir_lowering=False)

    def din(name, shape, dt=F32):
        return nc.dram_tensor(name, list(shape), dt, kind="ExternalInput").ap()

    x_d = din("x", [S, D])
    ctx_d = din("ctx", [CT, D])
    cT_d = din("cT", [128, 8, 2])
    wmod_d = din("wmod", [48, 128, 8, 128])
    bmodT_d = din("bmodT", [128, 48])
    win_d = din("win", [N_WIN_CH, 128, 8, 128])
    wuq_d = din("wuq", [128, 3, 1024])
    wukv_d = din("wukv", [128, 2, 1024])
    wpa_d = din("wpa", [8, 128, 4, 128])
    wpb_d = din("wpb", [8, 128, 4, 128])
    wout_d = din("wout", [128, 8, 1024])
    wup_d = din("wup", [NF, 128, 8, 256])
    wdown_d = din("wdown", [128, NF, 1024])
    tabA_d = din("tabA", [128, 2, S])
    tabB_d = din("tabB", [128, 2, S])
    gvec_d = din("gvec", [128, 16])
    lnrows_d = din("lnrows", [4, D])
    consts_d = din("consts", [128, 3, 128])
    out_d = nc.dram_tensor("out", [S, D], F32, kind="ExternalOutput").ap()
    dbg = {}
    if DEBUG:
        for nm, shp, dt in [("d_hT", [128, 8 * T], BF16), ("d_qa", [128, 4 * S], BF16), ("d_ka", [128, 2 * T], BF16),
                            ("d_va", [128, NKT * 320], BF16), ("d_cqn", [128, 3 * S], BF16),
                            ("d_ckvn", [128, 2 * T], BF16), ("d_kr", [128, T], BF16), ("d_modT", [128, 96], F32),
                            ("d_oa", [128, 4 * S], BF16), ("d_ob", [128, 4 * S], BF16), ("d_mT", [128, 8 * S], BF16),
                            ("d_x1", [128, 16 * D], F32), ("d_qb", [128, S], BF16), ("d_kb", [128, T], BF16),
                            ("d_vb", [128, NKT * 192], BF16)]:
            dbg[nm] = nc.dram_tensor(nm, shp, dt, kind="ExternalOutput").ap()

    from contextlib import ExitStack
    es = ExitStack()
    ARENA_KB = 190
    arena = es.enter_context(nc.sbuf_tensor("arena", [128, ARENA_KB * 512], BF16))

    def carve(off_kb, free_shape, dt):
        n = int(np.prod(free_shape))
        nb = n * (4 if dt == F32 else 2)
        o = int(round(off_kb * 1024))
        assert o % 4 == 0 and o + nb <= ARENA_KB * 1024, (off_kb, free_shape)
        ap = arena[:, o // 2:(o + nb) // 2]
        if dt == F32:
            ap = ap.bitcast(F32)
        if len(free_shape) == 2:
            ap = ap.rearrange("p (a b) -> p a b", b=free_shape[1])
        elif len(free_shape) == 3:
            ap = ap.rearrange("p (a b c) -> p a b c", b=free_shape[1], c=free_shape[2])
        return ap

    def sb(name, shape, dt):
        return es.enter_context(nc.sbuf_tensor("sb_" + name, list(shape), dt))

    consts = sb("consts", [128, 3, 128], BF16)
    identf = sb("identf", [128, 128], F32)
    onesf = sb("onesf", [128, 128], F32)
    gvec = sb("gvec", [128, 16], F32)
    cT = sb("cT", [128, 16], F32)
    scT = sb("scT", [128, 16], BF16)
    bmodT = sb("bmodT", [128, 48], F32)
    modT = sb("modT", [128, 2, 48], F32)
    opsc = sb("opsc", [128, 2, 16], F32)
    stat = sb("stat", [128, 8, 12], F32)
    mv = sb("mv", [128, 8, 2], F32)
    rstd = sb("rstd", [128, 8, 2], F32)
    bctmp = sb("bctmp", [128, 2, 128], F32)
    epsc = sb("epsc", [128, 1], F32)

    ident = consts[:, 0, :]
    blockones = consts[:, 1, :]
    ones128 = consts[:, 2, :]

    psw = [es.enter_context(nc.psum_tensor(f"psw{i}", [128, 1024], F32)) for i in range(4)]
    ps = [psw[i // 2][:, (i % 2) * 512:(i % 2 + 1) * 512] for i in range(8)]
    PS = [Buf(f"ps{i}", psum=True) for i in range(8)]

    sems = {e: es.enter_context(nc.semaphore(f"s_{e}")) for e in ("pe", "act", "dve", "pool")}
    _dsn = [0]

    _dpool = {"sp": [], "pool": []}

    def dsem():
        return DmaSem()

    def bind(ds, q):
        if ds.sem is not None:
            assert ds.q == q
            return
        ds.q = q
        if _dpool[q]:
            ds.sem, ds.count, ds.last = _dpool[q].pop()
            return
        _dsn[0] += 1
        assert _dsn[0] <= 56, "too many DMA semaphores"
        ds.sem = es.enter_context(nc.semaphore(f"d{_dsn[0]}"))

    def release(lst):
        for ds in lst:
            if ds.sem is not None:
                _dpool[ds.q].append((ds.sem, ds.count, ds.last))
                ds.sem = None

    K = Sched()

    def dma(q, out, in_, sem, reads=(), writes=()):
        bind(sem, q)
        return K.op(q, lambda e: e.dma_start(out=out, in_=in_), reads=reads, writes=writes, dma=sem)

    def mm(out, lhsT, rhs, start, stop, reads, writes):
        return K.op("pe", lambda e: e.matmul(out, lhsT=lhsT, rhs=rhs, start=start, stop=stop), reads=reads, writes=writes)

    def act(out, in_, func, reads, writes, scale=None, bias=None):
        kw = {}
        if scale is not None:
            kw["scale"] = scale
        if bias is not None:
            kw["bias"] = bias
        return K.op("act", lambda e: e.activation(out=out, in_=in_, func=func, **kw), reads=reads, writes=writes)

    def dve(fn, reads, writes):
        return K.op("dve", fn, reads=reads, writes=writes)

    def tt(out, in0, in1, op, reads, writes, eng="dve"):
        return K.op(eng, lambda e: e.tensor_tensor(out=out, in0=in0, in1=in1, op=op), reads=reads, writes=writes)

    def stt(out, in0, scalar, in1, op0, op1, reads, writes):
        return K.op("dve", lambda e: e.scalar_tensor_tensor(out=out, in0=in0, scalar=scalar, in1=in1, op0=op0, op1=op1),
                    reads=reads, writes=writes)

    def ts(out, in0, s1, s2, op0, op1, reads, writes):
        if op1 is None:
            return K.op("dve", lambda e: e.tensor_scalar(out=out, in0=in0, scalar1=s1, scalar2=None, op0=op0),
                        reads=reads, writes=writes)
        return K.op("dve", lambda e: e.tensor_scalar(out=out, in0=in0, scalar1=s1, scalar2=s2, op0=op0, op1=op1),
                    reads=reads, writes=writes)

    def cp(out, in_, reads, writes, eng="dve"):
        return K.op(eng, lambda e: e.tensor_copy(out=out, in_=in_), reads=reads, writes=writes)

    def recip(out, in_, reads, writes, fast=False):
        if fast and OPT_RF:
            return K.op("dve", lambda e: e.reciprocal_approx_fast(out=out, in_=in_), reads=reads, writes=writes)
        return K.op("dve", lambda e: e.reciprocal(out=out, in_=in_), reads=reads, writes=writes)

    B_const = Buf("const")
    B_cT = Buf("cT")
    s_c3 = dsem()
    dma("sp", cT[:].rearrange("p (k r) -> p k r", r=2), cT_d, s_c3, writes=[B_cT])
    s_c = dsem()
    dma("pool", consts[:], consts_d, s_c, writes=[B_const])
    s_c2 = dsem()
    dma("sp", gvec[:], gvec_d, s_c2, writes=[B_const])
    s_c4 = dsem()
    dma("sp", bmodT[:], bmodT_d, s_c4, writes=[B_const])
    s_c5 = dsem()
    dma("sp", identf[:], consts_d[:, 0, :], s_c5, writes=[B_const])
    B_c2 = Buf("const2")
    dve(lambda e: e.memset(onesf[:], 1.0), [], [B_c2])
    dve(lambda e: e.memset(epsc[:], EPS), [], [B_c2])
    B_scT = Buf("scT")
    act(scT[:], cT[:], AF.Silu, [B_cT], [B_scT])

    hT = carve(0, [8, T], BF16)
    qa = carve(36, [4, S], BF16)
    ka = carve(52, [2, T], BF16)
    va = carve(61, [NKT, 320], BF16)
    cqn = carve(72.5, [3, S], BF16)
    ckvn = carve(84.5, [2, T], BF16)
    kr = carve(93.5, [T], BF16)
    tabB = carve(98, [2, S], F32)
    tabA = carve(114, [2, S], F32)
    xstg = [carve(72.5 + 4 * i, [D], F32) for i in range(6)]
    xnb = [carve(142 + 2 * i, [D], BF16) for i in range(2)]
    wring = [carve(146 + 2 * i, [8, 128], BF16) for i in range(8)]
    sqb = carve(162, [3, 512], BF16)
    rsb = [carve(165 + 2 * i, [512], F32) for i in range(2)]
    t1b = [carve(169 + 2 * i, [512], F32) for i in range(2)]
    t2b = [carve(173 + 2 * i, [512], F32) for i in range(2)]
    wmodst = [carve(177 + 2 * i, [8, 128], BF16) for i in range(4)] + [carve(130 + 2 * i, [8, 128], BF16) for i in range(6)]
    NMODR = 10

    B_hT = [[Buf(f"hT{g}_{k}") for k in range(8)] for g in range(5)]
    B_tabA, B_tabB = Buf("tabA"), Buf("tabB")
    B_xstg = [Buf(f"xstg{i}") for i in range(6)]
    B_xnb = [Buf(f"xnb{i}") for i in range(2)]
    B_wring = [Buf(f"wring{i}") for i in range(8)]
    S_wring = [dsem() for _ in range(8)]
    S_xstg = [dsem() for _ in range(6)]
    B_wmodst = [Buf(f"wmodst{i}") for i in range(NMODR)]
    S_wmodst = [dsem() for _ in range(NMODR)]
    B_modT = Buf("modT")
    B_opsc = Buf("opsc")

    s_t = dsem()
    s_t2 = dsem()

    def load_tables():
        dma("sp", tabA, tabA_d, s_t, writes=[B_tabA])
        dma("sp", tabB[64:96], tabB_d[64:96], s_t2, writes=[B_tabB])

    PS_mod = PS[7]
    ps_mod = ps[7]

    W_P2 = []
    for c_ in range(4):
        W_P2 += [CH_Q + c_, CH_QR + c_]
    for c_ in range(2):
        W_P2 += [CH_K + c_, CH_KR + c_]
    W_P2 += [CH_V, CH_CQ, CH_CQ + 1, CH_CQ + 2, CH_CKV, CH_CKV + 1, CH_KRR]
    W_P4 = []
    for c_ in range(8):
        W_P4 += [CH_GA + c_, CH_GB + c_]
    w_state = {"order": W_P2, "issued": 0, "base": 0}

    def w_issue(upto):
        upto = min(upto, len(w_state["order"]))
        while w_state["issued"] < upto:
            i = w_state["issued"]
            slot = (w_state["base"] + i) % 8
            dma("pool", wring[slot], win_d[w_state["order"][i]], S_wring[slot], writes=[B_wring[slot]])
            w_state["issued"] += 1

    def load_w(chunk):
        i = w_state["order"].index(chunk)
        w_issue(i + 1)
        return (w_state["base"] + i) % 8

    def w_ahead(chunk):
        w_issue(w_state["order"].index(chunk) + 8)

    def w_prefetch_p2():
        w_issue(8)

    mod_issued = [0]

    def mod_issue(upto):
        while mod_issued[0] < upto:
            j = mod_issued[0]
            wi = j % NMODR
            dma("pool", wmodst[wi], wmod_d[j], S_wmodst[wi], writes=[B_wmodst[wi]])
            mod_issued[0] += 1

    def mod_mm(j, limit):
        mod_issue(min(limit, j + NMODR))
        wi = j % NMODR
        for k in range(8):
            mm(ps_mod[:, 2 * j:2 * j + 2], wmodst[wi][:, k, :], scT[:].rearrange("p (k r) -> p k r", r=2)[:, k, :],
               k == 0, k == 7, [B_wmodst[wi], B_scT], [PS_mod])

    def mod_chunks(j0, j1, limit, mm_done=False):
        if not mm_done:
            for j in range(j0, j1):
                mod_mm(j, limit)
        for r in range(2):
            src = ps_mod[:, 2 * j0:2 * j1].rearrange("p (j r) -> p r j", r=2)[:, r, :]
            tt(modT[:, r, j0:j1], src, bmodT[:, j0:j1], ALU.add, [PS_mod, B_const], [B_modT])

    mod_issue(NMODR)

    def ln_stats(src, slot, B_src, B_stat, defer_recip=False):
        for hh in range(2):
            dve(lambda e, hh=hh: e.bn_stats(out=stat[:, slot, 6 * hh:6 * hh + 6], in_=src[:, 512 * hh:512 * hh + 512]),
                [B_src], [B_stat])
        dve(lambda e: e.bn_aggr(out=mv[:, slot, :], in_=stat[:, slot, :]), [B_stat], [B_stat])
        act(rstd[:, slot, 0:1], mv[:, slot, 1:2], AF.Sqrt, [B_stat, B_c2], [B_stat], bias=epsc[:, 0:1])
        if not defer_recip:
            recip(rstd[:, slot, 1:2], rstd[:, slot, 0:1], [B_stat], [B_stat])

    def ln_recip(slot, B_stat):
        recip(rstd[:, slot, 1:2], rstd[:, slot, 0:1], [B_stat], [B_stat])

    B_stat = [Buf(f"stat{i}") for i in range(8)]
    LA = 3
    NT1 = 18

    def tile_info(ti):
        if ti < 2:
            return 0, ti
        return 1 + (ti - 2) // 4, (ti - 2) % 4

    for step in range(NT1 + LA):
        if step < NT1:
            ti = step
            slot, sl = ti % 6, ti % 8
            src = ctx_d[ti * 128:(ti + 1) * 128, :] if ti < 2 else x_d[(ti - 2) * 128:(ti - 1) * 128, :]
            dma("sp", xstg[slot], src, S_xstg[slot], writes=[B_xstg[slot]])
            ln_stats(xstg[slot], sl, B_xstg[slot], B_stat[sl], defer_recip=True)
            if step == 5:
                load_tables()
        if 1 <= step <= NT1:
            ln_recip((step - 1) % 8, B_stat[(step - 1) % 8])
        if step < 16:
            mod_mm(step, 16)
        j = step - LA
        if j < 0:
            continue
        ti = j
        g, s_ = tile_info(ti)
        slot, sl, nb = ti % 6, ti % 8, ti % 2
        banks = [0, 1, 2, 3] if g % 2 == 0 else [4, 5, 6, 3]
        ts(xnb[nb], xstg[slot], mv[:, sl, 0:1], rstd[:, sl, 1:2], ALU.subtract, ALU.mult,
           [B_xstg[slot], B_stat[sl]], [B_xnb[nb]])
        for k in range(8):
            bk = banks[k // 2]
            pv = ps[bk][:].bitcast(BF16)
            col = (k % 2) * 512 + s_ * 128
            K.op("pe", lambda e, pv=pv, col=col, k=k, nb=nb: e.transpose(pv[:, col:col + 128], xnb[nb][:, k * 128:(k + 1) * 128], ident),
                 reads=[B_xnb[nb], B_const], writes=[PS[bk]])
        ntile = 2 if g == 0 else 4
        if s_ == ntile - 1:
            n = ntile * 128
            t0 = 0 if g == 0 else CT + (g - 1) * 512
            r = 1 if g == 0 else 0
            for k in range(8):
                bk = banks[k // 2]
                pv = ps[bk][:].bitcast(BF16)
                col = (k % 2) * 512
                act(hT[:, k, t0:t0 + n], pv[:, col:col + n], AF.Identity, [PS[bk]], [B_hT[g][k]])

    mod_chunks(0, 16, 16, mm_done=True)
    w_prefetch_p2()
    for r in range(2):
        ts(opsc[:, r, 0:8], modT[:, r, 8:16], 1.0, None, ALU.add, None, [B_modT], [B_opsc])
    for g in range(5):
        n = CT if g == 0 else 512
        t0 = 0 if g == 0 else CT + (g - 1) * 512
        r = 1 if g == 0 else 0
        for k in range(8):
            K.op("dve", lambda e, k=k, t0=t0, n=n, r=r: e.tensor_scalar(out=hT[:, k, t0:t0 + n], in0=hT[:, k, t0:t0 + n],
                                                                    scalar1=opsc[:, r, k:k + 1], scalar2=modT[:, r, k:k + 1],
                                                                    op0=ALU.mult, op1=ALU.add),
                 reads=[B_opsc, B_modT], writes=[B_hT[g][k]])

    B_qa, B_ka, B_va, B_cqn, B_ckvn, B_kr = Buf("qa"), Buf("ka"), Buf("va"), Buf("cqn"), Buf("ckvn"), Buf("kr")
    B_sq = Buf("sq")
    B_rs = [Buf("rs0"), Buf("rs1")]
    B_t1 = [Buf("t10"), Buf("t11")]
    B_t2 = [Buf("t20"), Buf("t21")]
    groups = [(0, 0, CT)] + [(g, CT + (g - 1) * 512, 512) for g in range(1, 5)]
    pcount = [0]

    def nextps(n=1):
        r = []
        for _ in range(n):
            r.append(pcount[0] % 7)
            pcount[0] += 1
        return r

    def proj_fm(wi, g, t0, n, bank, m0=0, m1=128, prow0=0):
        for k in range(8):
            mm(ps[bank][prow0:prow0 + (m1 - m0), 0:n], wring[wi][:, k, m0:m1], hT[:, k, t0:t0 + n], k == 0, k == 7,
               [B_wring[wi], B_hT[g][k]], [PS[bank]])

    it = [0]

    def qk_unit(ch_p, ch_r, gcol, dst, dst_idx, B_dst, latent_only, nheaddim=64):
        wi_p = load_w(ch_p)
        wi_r = load_w(ch_r)
        w_ahead(ch_p)
        for (g, t0, n) in groups:
            if latent_only and g == 0:
                continue
            i2 = it[0] % 2
            it[0] += 1
            bp, br, bs = nextps(3)
            proj_fm(wi_p, g, t0, n, bp)
            act(sqb[:, 0, 0:n], ps[bp][:, 0:n], AF.Square, [PS[bp]], [B_sq])
            if g != 0:
                proj_fm(wi_r, g, t0, n, br)
            mm(ps[bs][:, 0:n], blockones, sqb[:, 0, 0:n], True, True, [B_sq, B_const], [PS[bs]])
            if OPT_LN:
                act(rsb[i2][:, 0:n], ps[bs][:, 0:n], AF.Ln, [PS[bs], B_c2], [B_rs[i2]], scale=1.0 / nheaddim, bias=epsc[:, 0:1])
                act(rsb[i2][:, 0:n], rsb[i2][:, 0:n], AF.Exp, [B_rs[i2]], [B_rs[i2]], scale=-0.5)
            else:
                act(rsb[i2][:, 0:n], ps[bs][:, 0:n], AF.Sqrt, [PS[bs], B_c2], [B_rs[i2]], scale=1.0 / nheaddim, bias=epsc[:, 0:1])
                recip(rsb[i2][:, 0:n], rsb[i2][:, 0:n], [B_rs[i2]], [B_rs[i2]], fast=True)
            tq0 = t0 - CT if not latent_only else t0 - CT
            if g == 0:
                stt(dst[:, dst_idx, t0:t0 + n], ps[bp][:, 0:n], gvec[:, gcol:gcol + 1], rsb[i2][:, 0:n], ALU.mult, ALU.mult,
                    [PS[bp], B_rs[i2], B_const], [B_dst])
                continue
            stt(t1b[i2][:, 0:n], ps[bp][:, 0:n], gvec[:, gcol:gcol + 1], tabA[:, 0, tq0:tq0 + n], ALU.mult, ALU.mult,
                [PS[bp], B_tabA, B_const], [B_t1[i2]])
            stt(t2b[i2][:, 0:n], ps[br][:, 0:n], gvec[:, gcol + 1:gcol + 2], tabA[:, 1, tq0:tq0 + n], ALU.mult, ALU.mult,
                [PS[br], B_tabA, B_const], [B_t2[i2]])
            tt(t1b[i2][:, 0:n], t1b[i2][:, 0:n], t2b[i2][:, 0:n], ALU.add, [B_t1[i2], B_t2[i2]], [B_t1[i2]], eng="pool")
            dcol0 = t0 - CT if latent_only else t0
            tt(dst[:, dst_idx, dcol0:dcol0 + n], t1b[i2][:, 0:n], rsb[i2][:, 0:n], ALU.mult, [B_t1[i2], B_rs[i2]], [B_dst], eng="pool")

    mod_next = [16]

    def mod_piece(nchunks=4):
        j0 = mod_next[0]
        j1 = min(48, j0 + nchunks)
        if j1 > j0:
            mod_chunks(j0, j1, 48)
            mod_next[0] = j1

    for c in range(4):
        qk_unit(CH_Q + c, CH_QR + c, 0, qa, c, B_qa, True)
        mod_piece()
    for kk in range(2):
        qk_unit(CH_K + kk, CH_KR + kk, 2, ka, kk, B_ka, False)
        mod_piece()

    for cc in (0, 128, 256):
        dve(lambda e, cc=cc: e.memset(va[:, :, cc:cc + 64], 1.0), [], [B_va])
    wi_v = load_w(CH_V)
    w_ahead(CH_V)
    for kt0 in range(0, NKT, 4):
        nk = min(4, NKT - kt0)
        (bv,) = nextps(1)
        for s_ in range(nk):
            kt = kt0 + s_
            g = 0 if kt < 2 else 1 + (kt - 2) // 4
            for k in range(8):
                mm(ps[bv][:, s_ * 128:(s_ + 1) * 128], hT[:, k, kt * 128:(kt + 1) * 128], wring[wi_v][:, k, :], k == 0, k == 7,
                   [B_wring[wi_v], B_hT[g][k]], [PS[bv]])
        src = ps[bv][:, 0:nk * 128].rearrange("p (s c) -> p s c", c=128)
        cp(va[:, kt0:kt0 + nk, 64:128], src[:, :, 0:64], [PS[bv]], [B_va])
        cp(va[:, kt0:kt0 + nk, 192:256], src[:, :, 64:128], [PS[bv]], [B_va])

    def lowrank_unit(ch0, nch, gcol0, dst, B_dst, latent_only, rank):
        wis = [load_w(ch0 + j) for j in range(nch)]
        w_ahead(ch0)
        for (g, t0, n) in groups:
            if latent_only and g == 0:
                continue
            i2 = it[0] % 2
            it[0] += 1
            banks = nextps(nch + 1)
            for j in range(nch):
                proj_fm(wis[j], g, t0, n, banks[j])
                act(sqb[:, j, 0:n], ps[banks[j]][:, 0:n], AF.Square, [PS[banks[j]]], [B_sq])
            bs = banks[nch]
            for j in range(nch):
                mm(ps[bs][:, 0:n], ones128, sqb[:, j, 0:n], j == 0, j == nch - 1, [B_sq, B_const], [PS[bs]])
            if OPT_LN:
                act(rsb[i2][:, 0:n], ps[bs][:, 0:n], AF.Ln, [PS[bs], B_c2], [B_rs[i2]], scale=1.0 / rank, bias=epsc[:, 0:1])
                act(rsb[i2][:, 0:n], rsb[i2][:, 0:n], AF.Exp, [B_rs[i2]], [B_rs[i2]], scale=-0.5)
            else:
                act(rsb[i2][:, 0:n], ps[bs][:, 0:n], AF.Sqrt, [PS[bs], B_c2], [B_rs[i2]], scale=1.0 / rank, bias=epsc[:, 0:1])
                recip(rsb[i2][:, 0:n], rsb[i2][:, 0:n], [B_rs[i2]], [B_rs[i2]], fast=True)
            dcol0 = t0 - CT if latent_only else t0
            for j in range(nch):
                stt(dst[:, j, dcol0:dcol0 + n], ps[banks[j]][:, 0:n], gvec[:, gcol0 + j:gcol0 + j + 1], rsb[i2][:, 0:n],
                    ALU.mult, ALU.mult, [PS[banks[j]], B_rs[i2], B_const], [B_dst] + B_xstg)

    mod_piece()
    lowrank_unit(CH_CQ, 3, 4, cqn, B_cqn, True, 384)
    mod_piece()
    lowrank_unit(CH_CKV, 2, 7, ckvn, B_ckvn, False, 256)

    wi_kr = load_w(CH_KRR)
    for (g, t0, n) in groups:
        i2 = it[0] % 2
        it[0] += 1
        bp, br = nextps(2)
        proj_fm(wi_kr, g, t0, n, bp, 0, 32, 64)
        if g == 0:
            cp(kr[64:96, t0:t0 + n], ps[bp][64:96, 0:n], [PS[bp]], [B_kr] + B_xstg)
            continue
        proj_fm(wi_kr, g, t0, n, br, 32, 64, 64)
        tq0 = t0 - CT
        tt(t1b[i2][64:96, 0:n], ps[bp][64:96, 0:n], tabB[64:96, 0, tq0:tq0 + n], ALU.mult, [PS[bp], B_tabB], [B_t1[i2]])
        tt(t2b[i2][64:96, 0:n], ps[br][64:96, 0:n], tabB[64:96, 1, tq0:tq0 + n], ALU.mult, [PS[br], B_tabB], [B_t2[i2]])
        tt(kr[64:96, t0:t0 + n], t1b[i2][64:96, 0:n], t2b[i2][64:96, 0:n], ALU.add, [B_t1[i2], B_t2[i2]], [B_kr] + B_xstg)

    mod_piece(48)
    ts(opsc[:, 0, 8:16], modT[:, 0, 32:40], 1.0, None, ALU.add, None, [B_modT], [B_opsc])

    out_sems = [dsem() for _ in range(4)]
    B_dbg = Buf("dbg")
    n_out = [0]

    def store(dst, src, reads):
        s = out_sems[n_out[0] % 4]
        n_out[0] += 1
        return dma("sp", dst, src, s, reads=reads, writes=[B_dbg])

    if DEBUG:
        store(dbg["d_hT"], hT.rearrange("p a b -> p (a b)"), [b for gg in B_hT for b in gg])
        store(dbg["d_qa"], qa.rearrange("p a b -> p (a b)"), [B_qa])
        store(dbg["d_ka"], ka.rearrange("p a b -> p (a b)"), [B_ka])
        store(dbg["d_va"], va.rearrange("p a b -> p (a b)"), [B_va])
        store(dbg["d_cqn"], cqn.rearrange("p a b -> p (a b)"), [B_cqn])
        store(dbg["d_ckvn"], ckvn.rearrange("p a b -> p (a b)"), [B_ckvn])
        store(dbg["d_kr"], kr, [B_kr])
        store(dbg["d_modT"], modT[:].rearrange("p a b -> p (a b)"), [B_modT])

    K.barrier(extra=[x.last for x in out_sems if x.last is not None])
    release(S_xstg + S_wmodst + [s_c, s_c2, s_c3, s_c4, s_c5, s_t, s_t2])
    oa = carve(114, [4, S], BF16)
    PT = [carve(130 + 2 * i, [1024], BF16) for i in range(3)]
    recb = [carve(136 + 2 * i, [512], F32) for i in range(2)]
    wuq = carve(140, [3, 1024], BF16)
    wukv = carve(146, [2, 1024], BF16)
    qb = [carve(150 + 4 * i, [S], BF16) for i in range(2)]
    kb = [carve(158 + 4.5 * i, [T], BF16) for i in range(2)]
    vb = [carve(167 + 6.75 * i, [NKT, 192], BF16) for i in range(2)]
    ob = carve(36, [4, S], BF16)
    r1b = [carve(181 + 2 * i, [512], F32) for i in range(2)]
    r2b = [carve(185 + 2 * i, [512], F32) for i in range(2)]
    B_r1 = [Buf("r10"), Buf("r11")]
    B_r2 = [Buf("r20"), Buf("r21")]
    B_oa, B_ob = Buf("oa"), Buf("ob")
    B_PT = [Buf(f"PT{i}") for i in range(3)]
    B_rec = [Buf("rec0"), Buf("rec1")]
    B_wuq, B_wukv = Buf("wuq"), Buf("wukv")
    B_qb = [Buf("qb0"), Buf("qb1")]
    B_kb = [Buf("kb0"), Buf("kb1")]
    B_vb = [Buf("vb0"), Buf("vb1")]
    s_wuq = [dsem() for _ in range(5)]
    for k in range(3):
        dma("pool", wuq[:, k, :], wuq_d[:, k, :], s_wuq[k], writes=[B_wuq])
    for k in range(2):
        dma("pool", wukv[:, k, :], wukv_d[:, k, :], s_wuq[3 + k], writes=[B_wukv])

    class Step:
        __slots__ = ("pre", "S", "E", "PV", "post")

        def __init__(self):
            self.pre = self.S = self.E = self.PV = self.post = None

    def run_pipeline(steps):
        n = len(steps)
        for i in range(n + 2):
            if i < n:
                st_ = steps[i]
                if st_.pre is not None:
                    st_.pre()
                st_.S()
                st_.E()
            if i >= 2:
                st_ = steps[i - 2]
                st_.PV()
                if st_.post is not None:
                    st_.post()

    def finalize(obank, par, dst, B_dst, c, qt, ri, on_act=False):
        tok = slice(qt * 512, (qt + 1) * 512)
        lo, hi = (slice(0, 64), slice(64, 128)) if par == 0 else (slice(64, 128), slice(0, 64))
        if on_act:
            act(recb[ri][lo, :], ps[obank][hi, :], AF.Ln, [PS[obank]], [B_rec[ri]])
            act(recb[ri][lo, :], recb[ri][lo, :], AF.Exp, [B_rec[ri]], [B_rec[ri]], scale=-1.0)
        else:
            recip(recb[ri][lo, :], ps[obank][hi, :], [PS[obank]], [B_rec[ri]], fast=True)
        wr = [B_dst, B_qa] if on_act else [B_dst]
        tt(dst[lo, c, tok], ps[obank][lo, :], recb[ri][lo, :], ALU.mult, [PS[obank], B_rec[ri]], wr)

    steps = []
    idx = 0
    for c in range(4):
        kvh = c // 2
        for qt in range(4):
            oi = c * 4 + qt
            ob0, ob1 = (4, 5) if oi % 2 == 0 else (6, 7)
            for kt in range(NKT):
                st_ = Step()
                sbk = idx % 2
                pi = idx % 3
                idx += 1
                b0, b1 = 2 * sbk, 2 * sbk + 1
                qs = slice(qt * 512, (qt + 1) * 512)
                ks = slice(kt * 128, (kt + 1) * 128)

                def S_(b0=b0, b1=b1, c=c, kvh=kvh, qs=qs, ks=ks):
                    mm(ps[b0][:, :], ka[0:64, kvh, ks], qa[0:64, c, qs], True, True, [B_qa, B_ka], [PS[b0]])
                    mm(ps[b1][:, :], ka[64:128, kvh, ks], qa[64:128, c, qs], True, True, [B_qa, B_ka], [PS[b1]])

                def E_(b0=b0, b1=b1, pi=pi, sbk=sbk):
                    act(PT[pi][:, 0:1024], psw[sbk][:, :], AF.Exp, [PS[b0], PS[b1]], [B_PT[pi]], scale=A_SCALE)

                def PV_(ob0=ob0, ob1=ob1, pi=pi, kt=kt, kvh=kvh):
                    mm(ps[ob0][:, :], va[:, kt, 64 + 128 * kvh:192 + 128 * kvh], PT[pi][:, 0:512], kt == 0, kt == NKT - 1,
                       [B_PT[pi], B_va], [PS[ob0]])
                    mm(ps[ob1][:, :], va[:, kt, 128 * kvh:128 + 128 * kvh], PT[pi][:, 512:1024], kt == 0, kt == NKT - 1,
                       [B_PT[pi], B_va], [PS[ob1]])

                st_.S, st_.E, st_.PV = S_, E_, PV_
                if kt == NKT - 1:
                    def post_(ob0=ob0, ob1=ob1, c=c, qt=qt, oi=oi):
                        finalize(ob0, 0, oa, B_oa, c, qt, oi % 2)
                        finalize(ob1, 1, oa, B_oa, c, qt, oi % 2)
                    st_.post = post_
                steps.append(st_)
    steps_A = steps

    jit_i = [0]

    def jitps():
        b = 6 + (jit_i[0] % 2)
        jit_i[0] += 1
        return b

    for bi in range(2):
        dve(lambda e, bi=bi: e.memset(vb[bi][:, :, 0:64], 1.0), [], [B_vb[bi]])
        dve(lambda e, bi=bi: e.memset(vb[bi][:, :, 128:192], 1.0), [], [B_vb[bi]])
        cp(kb[bi][64:96, :], kr[64:96, :], [B_kr], [B_kb[bi]])

    def jit_tasks(h):
        bi = h % 2
        tasks = []
        for (g, t0, n) in groups:
            def t_k(g=g, t0=t0, n=n):
                bk = jitps()
                for k in range(2):
                    mm(ps[bk][0:64, 0:n], wukv[:, k, h * 128:h * 128 + 64], ckvn[:, k, t0:t0 + n], k == 0, k == 1,
                       [B_wukv, B_ckvn], [PS[bk]])
                cp(kb[bi][0:64, t0:t0 + n], ps[bk][0:64, 0:n], [PS[bk]], [B_kb[bi]])
            tasks.append(t_k)
        for kt0 in range(0, NKT, 3):
            def t_v(kt0=kt0):
                nk = min(3, NKT - kt0)
                bk = jitps()
                for s_ in range(nk):
                    kt = kt0 + s_
                    for k in range(2):
                        mm(ps[bk][:, s_ * 64:(s_ + 1) * 64], ckvn[:, k, kt * 128:(kt + 1) * 128],
                           wukv[:, k, h * 128 + 64:h * 128 + 128], k == 0, k == 1, [B_wukv, B_ckvn], [PS[bk]])
                cp(vb[bi][:, kt0:kt0 + nk, 64:128], ps[bk][:, 0:nk * 64].rearrange("p (s c) -> p s c", c=64), [PS[bk]], [B_vb[bi]])
            tasks.append(t_v)
        for qt in range(4):
            def t_q1(qt=qt):
                bq = jitps()
                tok = slice(qt * 512, (qt + 1) * 512)
                i2 = qt % 2
                for k in range(3):
                    mm(ps[bq][0:96, :], wuq[:, k, h * 96:(h + 1) * 96], cqn[:, k, tok], k == 0, k == 2, [B_wuq, B_cqn], [PS[bq]])
                cp(qb[bi][0:64, tok], ps[bq][0:64, :], [PS[bq]], [B_qb[bi]])
                tt(r1b[i2][64:96, :], ps[bq][64:96, :], tabB[64:96, 0, tok], ALU.mult, [PS[bq], B_tabB], [B_r1[i2]])

            def t_q2(qt=qt):
                br = jitps()
                tok = slice(qt * 512, (qt + 1) * 512)
                i2 = qt % 2
                for k in range(3):
                    mm(ps[br][64:96, :], wuq[:, k, 768 + h * 32:768 + (h + 1) * 32], cqn[:, k, tok], k == 0, k == 2,
                       [B_wuq, B_cqn], [PS[br]])
                tt(r2b[i2][64:96, :], ps[br][64:96, :], tabB[64:96, 1, tok], ALU.mult, [PS[br], B_tabB], [B_r2[i2]])
                tt(qb[bi][64:96, tok], r1b[i2][64:96, :], r2b[i2][64:96, :], ALU.add, [B_r1[i2], B_r2[i2]], [B_qb[bi]], eng="pool")
            tasks.append(t_q1)
            tasks.append(t_q2)
        return tasks

    def jit_head(h):
        for t_ in jit_tasks(h):
            t_()

    pend0 = jit_tasks(1) + jit_tasks(0)
    a_slots = [st_ for (st_, (c_, qt_, kt_)) in zip(steps_A, [(c_, qt_, kt_) for c_ in range(4) for qt_ in range(4) for kt_ in range(NKT)])
               if c_ in (2, 3) and qt_ in (0, 2) and kt_ >= 2]
    assert len(a_slots) >= len(pend0)
    for st_, t_ in zip(a_slots, pend0):
        st_.pre = t_
    steps = []
    for h in range(8):
        bi = h % 2
        c, par = h // 2, h % 2
        for qt in range(4):
            oi = h * 4 + qt
            obk = 4 + (oi % 2)
            for kp in range(NKT // 2):
                st_ = Step()
                sbk = idx % 2
                pi = idx % 3
                idx += 1
                b0, b1 = 2 * sbk, 2 * sbk + 1
                qs = slice(qt * 512, (qt + 1) * 512)

                def S_(b0=b0, b1=b1, bi=bi, qs=qs, kp=kp):
                    for j, bj in enumerate((b0, b1)):
                        kt = 2 * kp + j
                        mm(ps[bj][:, :], kb[bi][0:96, kt * 128:(kt + 1) * 128], qb[bi][0:96, qs], True, True,
                           [B_qb[bi], B_kb[bi]], [PS[bj]])

                def E_(b0=b0, b1=b1, pi=pi, sbk=sbk):
                    act(PT[pi][:, 0:1024], psw[sbk][:, :], AF.Exp, [PS[b0], PS[b1]], [B_PT[pi]], scale=B_SCALE)

                def PV_(obk=obk, pi=pi, kp=kp, bi=bi, par=par):
                    for j in range(2):
                        kt = 2 * kp + j
                        lhs = vb[bi][:, kt, 64:192] if par == 0 else vb[bi][:, kt, 0:128]
                        mm(ps[obk][:, :], lhs, PT[pi][:, j * 512:(j + 1) * 512], kt == 0, kt == NKT - 1,
                           [B_PT[pi], B_vb[bi]], [PS[obk]])

                st_.S, st_.E, st_.PV = S_, E_, PV_
                if kp == NKT // 2 - 1:
                    def post_(obk=obk, par=par, c=c, qt=qt, oi=oi):
                        finalize(obk, par, ob, B_ob, c, qt, oi % 2, on_act=True)
                    st_.post = post_
                if 1 <= h and h + 1 < 8:
                    li = qt * (NKT // 2) + kp
                    if li == 0:
                        pending = jit_tasks(h + 1)
                    if li >= 2 and (li - 2) < len(pending):
                        st_.pre = pending[li - 2]
                steps.append(st_)
    run_pipeline(steps_A + steps)
    if DEBUG:
        store(dbg["d_oa"], oa.rearrange("p a b -> p (a b)"), [B_oa])
        store(dbg["d_ob"], ob.rearrange("p a b -> p (a b)"), [B_ob])
        store(dbg["d_qb"], qb[1], [B_qb[1]])
        store(dbg["d_kb"], kb[1], [B_kb[1]])
        store(dbg["d_vb"], vb[1].rearrange("p a b -> p (a b)"), [B_vb[1]])
    K.barrier(extra=[x.last for x in out_sems if x.last is not None])

    release(s_wuq)
    mT = carve(52, [8, S], BF16)
    sab = [carve(84 + 2 * i, [512], F32) for i in range(4)]
    mab = [carve(92 + 2 * i, [512], F32) for i in range(4)]
    pring = [carve(100 + 1 * i, [4, 128], BF16) for i in range(4)]
    wout = carve(164, [8, 1024], BF16)
    B_mT = Buf("mT")
    gtb = carve(130, [D], F32)
    B_gtb = Buf("gtb")
    B_bct = Buf("bctmp")

    def bcast_mod(dst, B_dst_, j0):
        for half in range(2):
            b0, = nextps(1)
            for jj in range(4):
                j = j0 + half * 4 + jj
                i2 = it[0] % 2
                it[0] += 1
                ts(bctmp[:, i2, :], onesf[:], modT[:, 0, j:j + 1], None, ALU.mult, None, [B_c2, B_modT], [B_bct])
                mm(ps[b0][:, jj * 128:(jj + 1) * 128], bctmp[:, i2, :], identf[:], True, True, [B_bct, B_const], [PS[b0]])
            cp(dst[:, half * 512:(half + 1) * 512], ps[b0][:, :], [PS[b0]], [B_dst_])

    B_sab = [Buf(f"sab{i}") for i in range(4)]
    B_mab = [Buf(f"mab{i}") for i in range(4)]
    B_pring = [Buf(f"pring{i}") for i in range(4)]
    S_pring = [dsem() for _ in range(4)]
    B_wout = Buf("wout")
    s_wout = [dsem() for _ in range(8)]
    pr_next = [0]
    w_state.update(order=W_P4, issued=0, base=0)
    for c in range(8):
        wga = load_w(CH_GA + c)
        wgb = load_w(CH_GB + c)
        w_ahead(CH_GA + c)
        pa_i = pr_next[0] % 4
        pr_next[0] += 1
        dma("pool", pring[pa_i], wpa_d[c], S_pring[pa_i], writes=[B_pring[pa_i]])
        pb_i = pr_next[0] % 4
        pr_next[0] += 1
        dma("pool", pring[pb_i], wpb_d[c], S_pring[pb_i], writes=[B_pring[pb_i]])
        if c == 0:
            for k in range(8):
                dma("pool", wout[:, k, :], wout_d[:, k, :], s_wout[k], writes=[B_wout])
        if c == 1:
            bcast_mod(gtb, B_gtb, 16)
            for k in range(8):
                tt(wout[:, k, :], wout[:, k, :], gtb, ALU.mult, [B_wout, B_gtb], [B_wout], eng="pool")
        for qt in range(4):
            g = qt + 1
            t0 = CT + qt * 512
            tok = slice(qt * 512, (qt + 1) * 512)
            i2 = it[0] % 2
            it[0] += 1
            bga, bgb, bpa, bpb = nextps(4)
            proj_fm(wga, g, t0, 512, bga)
            proj_fm(wgb, g, t0, 512, bgb)
            for k in range(4):
                mm(ps[bpa][:, :], pring[pa_i][:, k, :], oa[:, k, tok], k == 0, k == 3, [B_pring[pa_i], B_oa], [PS[bpa]])
            for k in range(4):
                mm(ps[bpb][:, :], pring[pb_i][:, k, :], ob[:, k, tok], k == 0, k == 3, [B_pring[pb_i], B_ob], [PS[bpb]])
            act(sab[i2][:, :], ps[bga][:, :], AF.Sigmoid, [PS[bga]], [B_sab[i2]])
            act(sab[2 + i2][:, :], ps[bgb][:, :], AF.Sigmoid, [PS[bgb]], [B_sab[2 + i2]])
            tt(mab[i2][:, :], sab[i2][:, :], ps[bpa][:, :], ALU.mult, [B_sab[i2], PS[bpa]], [B_mab[i2]])
            tt(mab[2 + i2][:, :], sab[2 + i2][:, :], ps[bpb][:, :], ALU.mult, [B_sab[2 + i2], PS[bpb]], [B_mab[2 + i2]])
            tt(mT[:, c, tok], mab[i2][:, :], mab[2 + i2][:, :], ALU.add, [B_mab[i2], B_mab[2 + i2]], [B_mT])
    if DEBUG:
        store(dbg["d_mT"], mT.rearrange("p a b -> p (a b)"), [B_mT])
    K.barrier(extra=[x.last for x in out_sems if x.last is not None])

    release(S_pring + S_wring)
    x1 = carve(100, [16, D], F32)
    bc1 = [carve(0 + 4 * i, [D], F32) for i in range(3)]
    xs2 = [carve(12 + 4 * i, [D], F32) for i in range(2)]
    zb = [carve(20 + 4 * i, [D], F32) for i in range(2)]
    B_bc1 = [Buf(f"bc1_{i}") for i in range(3)]
    B_xs2 = [Buf("xs2_0"), Buf("xs2_1")]
    S_xs2 = [dsem(), dsem()]
    B_zb = [Buf("zb0"), Buf("zb1")]
    B_x1 = [Buf(f"x1_{i}") for i in range(16)]
    s_ln = [dsem() for _ in range(4)]

    dma("sp", bc1[1], lnrows_d[0:1, :].partition_broadcast(128), s_ln[0], writes=[B_bc1[1]])
    dma("sp", bc1[2], lnrows_d[1:2, :].partition_broadcast(128), s_ln[1], writes=[B_bc1[2]])

    def post_norm(y_banks, xsrc, B_xsrc, gt_bc, B_gt, g_bc, B_g, b_bc, B_b, dst, B_dst_, zi, sl):
        z = zb[zi]
        for hh in range(2):
            hs = slice(hh * 512, (hh + 1) * 512)
            stt(z[:, hs], xsrc[:, hs], ALPHA, ps[y_banks[hh]][:, :], ALU.mult, ALU.add,
                [B_xsrc, PS[y_banks[hh]]], [B_zb[zi]])
        ln_stats(z, sl, B_zb[zi], B_stat[sl])
        ts(z, z, mv[:, sl, 0:1], rstd[:, sl, 1:2], ALU.subtract, ALU.mult, [B_zb[zi], B_stat[sl]], [B_zb[zi]])
        tt(z, z, g_bc, ALU.mult, [B_zb[zi], B_g], [B_zb[zi]], eng="pool")
        tt(dst, z, b_bc, ALU.add, [B_zb[zi], B_b], [B_dst_], eng="pool")

    for st in range(16):
        xi = st % 2
        dma("sp", xs2[xi], x_d[st * 128:(st + 1) * 128, :], S_xs2[xi], writes=[B_xs2[xi]])
        yb = nextps(2)
        for hh in range(2):
            for k in range(8):
                mm(ps[yb[hh]][:, :], mT[:, k, st * 128:(st + 1) * 128], wout[:, k, hh * 512:(hh + 1) * 512], k == 0, k == 7,
                   [B_mT, B_wout], [PS[yb[hh]]])
        post_norm(yb, xs2[xi], B_xs2[xi], bc1[0], B_bc1[0], bc1[1], B_bc1[1], bc1[2], B_bc1[2], x1[:, st, :], B_x1[st], st % 2, st % 4)
    if DEBUG:
        store(dbg["d_x1"], x1.rearrange("p a b -> p (a b)"), B_x1)
    K.barrier(extra=[x.last for x in out_sems if x.last is not None])

    release(s_wout + S_xs2)
    wdown = carve(0, [NF, 1024], BF16)
    actT = carve(44, [NF, 512], BF16)
    h2T = [carve(66 + 8 * i, [8, 512], BF16) for i in range(2)]
    upring = [carve(168, [8, 256], BF16), carve(180, [8, 256], BF16), carve(184, [8, 256], BF16)]
    bc2 = [carve(82 + 4 * i, [D], F32) for i in range(3)]
    xn2 = [carve(94 + 2 * i, [D], BF16) for i in range(2)]
    sa2 = [carve(164 + 2 * i, [512], F32) for i in range(2)]
    zb2 = [carve(172 + 4 * i, [D], F32) for i in range(2)]
    NUP = 3
    B_wdown = [Buf(f"wdown{j}") for j in range(NF)]
    s_wdown = [dsem() for _ in range(4)]
    B_actT = Buf("actT")
    B_h2T = [Buf("h2T0"), Buf("h2T1")]
    B_upring = [Buf(f"up{i}") for i in range(NUP)]
    S_upring = [dsem() for _ in range(NUP)]
    B_bc2 = [Buf(f"bc2_{i}") for i in range(3)]
    B_xn2 = [Buf("xn2_0"), Buf("xn2_1")]
    B_sa2 = [Buf("sa2_0"), Buf("sa2_1")]
    B_zb2 = [Buf(f"zb2_{i}") for i in range(2)]

    bcast_mod(bc2[0], B_bc2[0], 40)
    dma("sp", bc2[1], lnrows_d[2:3, :].partition_broadcast(128), s_ln[2], writes=[B_bc2[1]])
    dma("sp", bc2[2], lnrows_d[3:4, :].partition_broadcast(128), s_ln[3], writes=[B_bc2[2]])

    up_next = [0]
    wdown_loaded = [False]
    oz = [0]

    def ln2_dve(tg):
        pass

    def ln2_group(tg, banks):
        hb = tg % 2
        for s_ in range(4):
            st = tg * 4 + s_
            sl = 4 + st % 4
            ln_stats(x1[:, st, :], sl, B_x1[st], B_stat[sl], defer_recip=True)
        for s_ in range(4):
            st = tg * 4 + s_
            sl = 4 + st % 4
            nb = st % 2
            ln_recip(sl, B_stat[sl])
            ts(xn2[nb], x1[:, st, :], mv[:, sl, 0:1], rstd[:, sl, 1:2], ALU.subtract, ALU.mult, [B_x1[st], B_stat[sl]], [B_xn2[nb]])
            for k in range(8):
                bk = banks[k // 2]
                pv = ps[bk][:].bitcast(BF16)
                col = (k % 2) * 512 + s_ * 128
                K.op("pe", lambda e, pv=pv, col=col, k=k, nb=nb: e.transpose(pv[:, col:col + 128], xn2[nb][:, k * 128:(k + 1) * 128], ident),
                     reads=[B_xn2[nb], B_const], writes=[PS[bk]])
        for k in range(8):
            bk = banks[k // 2]
            pv = ps[bk][:].bitcast(BF16)
            col = (k % 2) * 512
            act(h2T[hb][:, k, :], pv[:, col:col + 512], AF.Identity, [PS[bk], B_opsc, B_modT], [B_h2T[hb]],
                scale=opsc[:, 0, 8 + k:9 + k], bias=modT[:, 0, 24 + k:25 + k])

    up_issued = [0]
    NUPQ = 4 * NF

    def up_issue(upto):
        upto = min(upto, NUPQ)
        while up_issued[0] < upto:
            q = up_issued[0]
            ui = q % NUP
            dma("pool", upring[ui], wup_d[q % NF], S_upring[ui], writes=[B_upring[ui]])
            up_issued[0] += 1
            if not wdown_loaded[0] and q == 1:
                for jj in range(NF):
                    dma("pool", wdown[:, jj, :], wdown_d[:, jj, :], s_wdown[jj % 4], writes=[B_wdown[jj]])
                wdown_loaded[0] = True

    def up_group(tg):
        hb = tg % 2
        for j in range(NF):
            q = tg * NF + j
            up_issue(q + NUP)
            ui = q % NUP
            ba, bu = (4, 5) if j % 2 == 0 else (6, 7)
            for k in range(8):
                mm(ps[ba][:, :], upring[ui][:, k, 0:128], h2T[hb][:, k, :], k == 0, k == 7, [B_upring[ui], B_h2T[hb]], [PS[ba]])
            for k in range(8):
                mm(ps[bu][:, :], upring[ui][:, k, 128:256], h2T[hb][:, k, :], k == 0, k == 7, [B_upring[ui], B_h2T[hb]], [PS[bu]])
            i2 = j % 2
            act(sa2[i2][:, :], ps[ba][:, :], AF.Silu, [PS[ba]], [B_sa2[i2]])
            tt(actT[:, j, :], sa2[i2][:, :], ps[bu][:, :], ALU.mult, [B_sa2[i2], PS[bu]], [B_actT])
        up_issue((tg + 1) * NF + NUP)

    def down_sub(tg, s_):
        st = tg * 4 + s_
        yb = (0, 1) if s_ % 2 == 0 else (2, 3)
        for hh in range(2):
            for j in range(NF):
                mm(ps[yb[hh]][:, :], actT[:, j, s_ * 128:(s_ + 1) * 128], wdown[:, j, hh * 512:(hh + 1) * 512], j == 0, j == NF - 1,
                   [B_actT, B_wdown[j]], [PS[yb[hh]]])
        return yb

    def post2(tg, s_, yb):
        st = tg * 4 + s_
        zi = oz[0] % 2
        oz[0] += 1
        z = zb2[zi]
        sl = st % 4
        for hh in range(2):
            tt(z[:, hh * 512:(hh + 1) * 512], ps[yb[hh]][:, :], bc2[0][:, hh * 512:(hh + 1) * 512], ALU.mult,
               [PS[yb[hh]], B_bc2[0]], [B_zb2[zi]])
        stt(z, x1[:, st, :], ALPHA, z, ALU.mult, ALU.add, [B_x1[st], B_zb2[zi]], [B_zb2[zi]])
        ln_stats(z, sl, B_zb2[zi], B_stat[sl])
        ts(z, z, mv[:, sl, 0:1], rstd[:, sl, 1:2], ALU.subtract, ALU.mult, [B_zb2[zi], B_stat[sl]], [B_zb2[zi]])
        tt(z, z, bc2[1], ALU.mult, [B_zb2[zi], B_bc2[1]], [B_zb2[zi]], eng="pool")
        tt(z, z, bc2[2], ALU.add, [B_zb2[zi], B_bc2[2]], [B_zb2[zi]], eng="pool")
        store(out_d[st * 128:(st + 1) * 128, :], z, [B_zb2[zi]])

    ln2_group(0, [0, 1, 2, 3])
    for tg in range(4):
        up_group(tg)
        for s_ in range(4):
            yb = down_sub(tg, s_)
            if s_ == 0 and tg + 1 < 4:
                ln2_group(tg + 1, [4, 5, 6, 7])
            post2(tg, s_, yb)

    fin = Op("sp", None, None)
    for s in out_sems:
        if s.last is not None:
            fin.deps.append(s.last)
    K.ops["sp"].append(fin)

    with nc.Block() as block:
        def run(ename):
            def f(e):
                _emit_one(K, ename, e, sems)
            return f

        block.tensor(run("pe"))
        block.scalar(run("act"))
        block.vector(run("dve"))
        block.gpsimd(run("pool"))
        block.sync(run("sp"))
    es.close()
    return nc


def _prepare_vals(K, sems):
    for e in K.ENGS:
        c = 0
        for o in K.ops[e]:
            if o.dma is None and o.sig and o.fn is not None:
                c += 1
                o.sem = sems[e]
                o.val = c


def _emit_one(K, ename, eng, sems):
    if not getattr(K, "_prepared", False):
        _prepare_vals(K, sems)
        K._prepared = True
    waited = {}
    for o in K.ops[ename]:
        for d in o.deps:
            key = id(d.sem)
            if waited.get(key, 0) < d.val:
                eng.wait_ge(d.sem, d.val)
                waited[key] = d.val
        if o.fn is None:
            continue
        inst = o.fn(eng)
        if o.sig:
            inst.then_inc(o.sem, 16 if o.dma is not None else 1)


def _rope_tables():
    theta = np.float32(10000.0)
    rows = np.repeat(np.arange(32), 64).astype(np.float32)
    cols = np.tile(np.arange(64), 32).astype(np.float32)

    def tab(dim):
        d2 = dim // 2
        half = d2 // 2
        fr = (theta ** (-np.arange(half, dtype=np.float32) / np.float32(half))).astype(np.float32)
        cos = np.zeros((dim, S), np.float32)
        sin = np.zeros((dim, S), np.float32)
        for d in range(dim):
            pos = rows if d < d2 else cols
            i = (d % d2) % half
            ang = (pos * fr[i]).astype(np.float32)
            cos[d] = np.cos(ang)
            sgn = -1.0 if (d % d2) < half else 1.0
            sin[d] = sgn * np.sin(ang)
        return cos, sin

    cA, sA = tab(64)
    cB, sB = tab(32)
    tabA = np.stack([np.tile(cA, (2, 1)), np.tile(sA, (2, 1))], axis=1)
    tabB = np.stack([np.tile(cB, (4, 1)), np.tile(sB, (4, 1))], axis=1)
    return np.ascontiguousarray(tabA), np.ascontiguousarray(tabB)


def _perm(dim):
    d2 = dim // 2
    half = d2 // 2
    p = np.arange(dim)
    for d in range(dim):
        p[d] = d + half if (d % d2) < half else d - half
    return p


def _chunk_kp(w):
    m = w.shape[1] // 128
    return np.ascontiguousarray(w.reshape(8, 128, m, 128).transpose(2, 1, 0, 3))


def _host_layout(inp):
    f = lambda k: np.asarray(inp[k], dtype=np.float32)
    w_in = f("w_in")[0]
    pA, pB = _perm(64), _perm(32)
    q = w_in[:, 0:512]
    kk = w_in[:, 512:640]
    v = w_in[:, 640:768]
    cq = w_in[:, 768:1152]
    ckv = w_in[:, 1152:1408]
    krw = w_in[:, 1408:1440]
    ga = w_in[:, 1440:2464]
    gb = w_in[:, 2464:3488]
    q_rot = q.reshape(D, 8, 64)[:, :, pA].reshape(D, 512)
    k0, k1 = kk[:, 0:64], kk[:, 64:128]
    kdup = np.concatenate([k0, k0, k1, k1], axis=1)
    kdup_rot = np.concatenate([k0[:, pA], k0[:, pA], k1[:, pA], k1[:, pA]], axis=1)
    krc = np.concatenate([krw, krw[:, pB], np.zeros((D, 64), np.float32)], axis=1)
    ext = np.concatenate([q, q_rot, kdup, kdup_rot, v, cq, ckv, krc, ga, gb], axis=1)
    assert ext.shape[1] == N_WIN_CH * 128
    shared = {}
    shared["win"] = _chunk_kp(ext)
    shared["wmod"] = _chunk_kp(f("w_mod")[0])
    shared["bmodT"] = np.ascontiguousarray(f("b_mod")[0].reshape(48, 128).T)
    w_uq = f("w_uq")[0]
    uq_rot = w_uq.reshape(384, 8, 96)[:, :, 64:96][:, :, pB].reshape(384, 256)
    shared["wuq"] = np.ascontiguousarray(np.concatenate([w_uq, uq_rot], axis=1).reshape(3, 128, 1024).transpose(1, 0, 2))
    shared["wukv"] = np.ascontiguousarray(f("w_ukv")[0].reshape(2, 128, 1024).transpose(1, 0, 2))
    shared["wpa"] = np.ascontiguousarray(f("w_proj_a")[0].reshape(4, 128, 8, 128).transpose(2, 1, 0, 3))
    shared["wpb"] = np.ascontiguousarray(f("w_proj_b")[0].reshape(4, 128, 8, 128).transpose(2, 1, 0, 3))
    shared["wout"] = np.ascontiguousarray(f("w_out")[0].reshape(8, 128, 1024).transpose(1, 0, 2))
    w_up = f("w_up")[0]
    a4 = w_up[:, :FH].reshape(8, 128, NF, 128)
    u4 = w_up[:, FH:].reshape(8, 128, NF, 128)
    shared["wup"] = np.ascontiguousarray(np.concatenate([a4, u4], axis=3).transpose(2, 1, 0, 3))
    shared["wdown"] = np.ascontiguousarray(f("w_down")[0].reshape(NF, 128, 1024).transpose(1, 0, 2))
    tabA, tabB = _rope_tables()
    shared["tabA"], shared["tabB"] = tabA, tabB
    gv = np.zeros((128, 16), np.float32)
    qg, kg = f("q_norm_a")[0], f("k_norm_a")[0]
    gv[:, 0] = np.tile(qg, 2)
    gv[:, 1] = np.tile(qg[pA], 2)
    gv[:, 2] = np.tile(kg, 2)
    gv[:, 3] = np.tile(kg[pA], 2)
    gv[:, 4:7] = f("cq_norm")[0].reshape(3, 128).T
    gv[:, 7:9] = f("ckv_norm")[0].reshape(2, 128).T
    shared["gvec"] = gv
    shared["lnrows"] = np.ascontiguousarray(np.stack([f("ln1_g")[0], f("ln1_b")[0], f("ln2_g")[0], f("ln2_b")[0]]))
    cst = np.zeros((128, 3, 128), np.float32)
    cst[:, 0, :] = np.eye(128, dtype=np.float32)
    cst[0:64, 1, 0:64] = 1.0
    cst[64:128, 1, 64:128] = 1.0
    cst[:, 2, :] = 1.0
    shared["consts"] = cst
    x, c, ctx, c_ctx = f("x"), f("c"), f("ctx"), f("c_ctx")
    in_maps = []
    for b in range(8):
        m = dict(shared)
        m["x"] = np.ascontiguousarray(x[b])
        m["ctx"] = np.ascontiguousarray(ctx[b])
        m["cT"] = np.ascontiguousarray(np.stack([c[b].reshape(8, 128).T, c_ctx.reshape(8, 128).T], axis=2))
        in_maps.append(m)
    return in_maps


_NC_CACHE = {}


def kernel(**inputs):
    in_maps = _host_layout(inputs)
    if "nc" not in _NC_CACHE:
        _NC_CACHE["nc"] = build_program()
    nc = _NC_CACHE["nc"]
    res = run_bass_kernel_spmd(nc, in_maps, core_ids=list(range(8)))
    out = np.stack([np.asarray(r["out"], dtype=np.float32) for r in res.results], axis=0)
    if DEBUG:
        kernel.last_results = res.results
    return out
```

```python
import numpy as np
import concourse.bass as bass
import concourse.mybir as mybir
from concourse.bass_utils import run_bass_kernel_spmd

F32 = mybir.dt.float32
BF16 = mybir.dt.bfloat16
AF = mybir.ActivationFunctionType
ALU = mybir.AluOpType

D = 1024
S = 2048
CT = 256
T = S + CT
NKT = T // 128
FH = 2816
NF = FH // 128
EPS = 1e-6
ALPHA = 2.0 ** 0.25
A_SCALE = 64 ** -0.5
B_SCALE = 96 ** -0.5
DEBUG = False
import os as _os
OPT_LN = True
OPT_RF = False

CH_Q, CH_QR, CH_K, CH_KR, CH_V, CH_CQ, CH_CKV, CH_KRR, CH_GA, CH_GB = 0, 4, 8, 10, 12, 13, 16, 18, 19, 27
N_WIN_CH = 35


class Buf:
    __slots__ = ("name", "w", "r", "psum")

    def __init__(self, name, psum=False):
        self.name = name
        self.w = None
        self.r = []
        self.psum = psum


class DmaSem:
    __slots__ = ("sem", "count", "last", "q")

    def __init__(self, sem=None):
        self.sem = sem
        self.count = 0
        self.last = None
        self.q = None


class Op:
    __slots__ = ("eng", "fn", "deps", "sig", "sem", "val", "dma")

    def __init__(self, eng, fn, dma):
        self.eng = eng
        self.fn = fn
        self.deps = []
        self.sig = False
        self.sem = None
        self.val = 0
        self.dma = dma


class Sched:
    ENGS = ("pe", "act", "dve", "pool", "sp")

    def __init__(self):
        self.ops = {e: [] for e in self.ENGS}

    def op(self, eng, fn, reads=(), writes=(), dma=None):
        o = Op(eng, fn, dma)
        deps = []
        for b in reads:
            if b.w is not None:
                deps.append(b.w)
            if b.psum:
                deps.extend(r_ for r_ in b.r if r_.eng != eng)
        for b in writes:
            if b.w is not None:
                deps.append(b.w)
            deps.extend(b.r)
        if dma is not None:
            if dma.last is not None:
                deps.append(dma.last)
            dma.count += 1
            dma.last = o
            o.sem = dma.sem
            o.val = 16 * dma.count
            o.sig = True
        seen = set()
        for d in deps:
            if d is o or id(d) in seen:
                continue
            seen.add(id(d))
            if d.dma is None and d.eng == "pe" and eng == "pe":
                continue
            o.deps.append(d)
            d.sig = True
        for b in reads:
            b.r.append(o)
        for b in writes:
            b.w = o
            b.r = []
        self.ops[eng].append(o)
        return o

    def barrier(self, extra=()):
        marks = list(extra)
        for e in ("pe", "act", "dve"):
            for o in reversed(self.ops[e]):
                if o.fn is not None and o.dma is None:
                    marks.append(o)
                    break
        for e in self.ENGS:
            o = Op(e, None, None)
            for m in marks:
                o.deps.append(m)
                m.sig = True
            self.ops[e].append(o)


def build_program():
    nc = bass.Bass("TRN2", target_bir_lowering=False)

    def din(name, shape, dt=F32):
        return nc.dram_tensor(name, list(shape), dt, kind="ExternalInput").ap()

    x_d = din("x", [S, D])
    ctx_d = din("ctx", [CT, D])
    cT_d = din("cT", [128, 8, 2])
    wmod_d = din("wmod", [48, 128, 8, 128])
    bmodT_d = din("bmodT", [128, 48])
    win_d = din("win", [N_WIN_CH, 128, 8, 128])
    wuq_d = din("wuq", [128, 3, 1024])
    wukv_d = din("wukv", [128, 2, 1024])
    wpa_d = din("wpa", [8, 128, 4, 128])
    wpb_d = din("wpb", [8, 128, 4, 128])
    wout_d = din("wout", [128, 8, 1024])
    wup_d = din("wup", [NF, 128, 8, 256])
    wdown_d = din("wdown", [128, NF, 1024])
    tabA_d = din("tabA", [128, 2, S])
    tabB_d = din("tabB", [128, 2, S])
    gvec_d = din("gvec", [128, 16])
    lnrows_d = din("lnrows", [4, D])
    consts_d = din("consts", [128, 3, 128])
    out_d = nc.dram_tensor("out", [S, D], F32, kind="ExternalOutput").ap()
    dbg = {}
    if DEBUG:
        for nm, shp, dt in [("d_hT", [128, 8 * T], BF16), ("d_qa", [128, 4 * S], BF16), ("d_ka", [128, 2 * T], BF16),
                            ("d_va", [128, NKT * 320], BF16), ("d_cqn", [128, 3 * S], BF16),
                            ("d_ckvn", [128, 2 * T], BF16), ("d_kr", [128, T], BF16), ("d_modT", [128, 96], F32),
                            ("d_oa", [128, 4 * S], BF16), ("d_ob", [128, 4 * S], BF16), ("d_mT", [128, 8 * S], BF16),
                            ("d_x1", [128, 16 * D], F32), ("d_qb", [128, S], BF16), ("d_kb", [128, T], BF16),
                            ("d_vb", [128, NKT * 192], BF16)]:
            dbg[nm] = nc.dram_tensor(nm, shp, dt, kind="ExternalOutput").ap()

    from contextlib import ExitStack
    es = ExitStack()
    ARENA_KB = 190
    arena = es.enter_context(nc.sbuf_tensor("arena", [128, ARENA_KB * 512], BF16))

    def carve(off_kb, free_shape, dt):
        n = int(np.prod(free_shape))
        nb = n * (4 if dt == F32 else 2)
        o = int(round(off_kb * 1024))
        assert o % 4 == 0 and o + nb <= ARENA_KB * 1024, (off_kb, free_shape)
        ap = arena[:, o // 2:(o + nb) // 2]
        if dt == F32:
            ap = ap.bitcast(F32)
        if len(free_shape) == 2:
            ap = ap.rearrange("p (a b) -> p a b", b=free_shape[1])
        elif len(free_shape) == 3:
            ap = ap.rearrange("p (a b c) -> p a b c", b=free_shape[1], c=free_shape[2])
        return ap

    def sb(name, shape, dt):
        return es.enter_context(nc.sbuf_tensor("sb_" + name, list(shape), dt))

    consts = sb("consts", [128, 3, 128], BF16)
    identf = sb("identf", [128, 128], F32)
    onesf = sb("onesf", [128, 128], F32)
    gvec = sb("gvec", [128, 16], F32)
    cT = sb("cT", [128, 16], F32)
    scT = sb("scT", [128, 16], BF16)
    bmodT = sb("bmodT", [128, 48], F32)
    modT = sb("modT", [128, 2, 48], F32)
    opsc = sb("opsc", [128, 2, 16], F32)
    stat = sb("stat", [128, 8, 12], F32)
    mv = sb("mv", [128, 8, 2], F32)
    rstd = sb("rstd", [128, 8, 2], F32)
    bctmp = sb("bctmp", [128, 2, 128], F32)
    epsc = sb("epsc", [128, 1], F32)

    ident = consts[:, 0, :]
    blockones = consts[:, 1, :]
    ones128 = consts[:, 2, :]

    psw = [es.enter_context(nc.psum_tensor(f"psw{i}", [128, 1024], F32)) for i in range(4)]
    ps = [psw[i // 2][:, (i % 2) * 512:(i % 2 + 1) * 512] for i in range(8)]
    PS = [Buf(f"ps{i}", psum=True) for i in range(8)]

    sems = {e: es.enter_context(nc.semaphore(f"s_{e}")) for e in ("pe", "act", "dve", "pool")}
    _dsn = [0]

    _dpool = {"sp": [], "pool": []}

    def dsem():
        return DmaSem()

    def bind(ds, q):
        if ds.sem is not None:
            assert ds.q == q
            return
        ds.q = q
        if _dpool[q]:
            ds.sem, ds.count, ds.last = _dpool[q].pop()
            return
        _dsn[0] += 1
        assert _dsn[0] <= 56, "too many DMA semaphores"
        ds.sem = es.enter_context(nc.semaphore(f"d{_dsn[0]}"))

    def release(lst):
        for ds in lst:
            if ds.sem is not None:
                _dpool[ds.q].append((ds.sem, ds.count, ds.last))
                ds.sem = None

    K = Sched()

    def dma(q, out, in_, sem, reads=(), writes=()):
        bind(sem, q)
        return K.op(q, lambda e: e.dma_start(out=out, in_=in_), reads=reads, writes=writes, dma=sem)

    def mm(out, lhsT, rhs, start, stop, reads, writes):
        return K.op("pe", lambda e: e.matmul(out, lhsT=lhsT, rhs=rhs, start=start, stop=stop), reads=reads, writes=writes)

    def act(out, in_, func, reads, writes, scale=None, bias=None):
        kw = {}
        if scale is not None:
            kw["scale"] = scale
        if bias is not None:
            kw["bias"] = bias
        return K.op("act", lambda e: e.activation(out=out, in_=in_, func=func, **kw), reads=reads, writes=writes)

    def dve(fn, reads, writes):
        return K.op("dve", fn, reads=reads, writes=writes)

    def tt(out, in0, in1, op, reads, writes, eng="dve"):
        return K.op(eng, lambda e: e.tensor_tensor(out=out, in0=in0, in1=in1, op=op), reads=reads, writes=writes)

    def stt(out, in0, scalar, in1, op0, op1, reads, writes):
        return K.op("dve", lambda e: e.scalar_tensor_tensor(out=out, in0=in0, scalar=scalar, in1=in1, op0=op0, op1=op1),
                    reads=reads, writes=writes)

    def ts(out, in0, s1, s2, op0, op1, reads, writes):
        if op1 is None:
            return K.op("dve", lambda e: e.tensor_scalar(out=out, in0=in0, scalar1=s1, scalar2=None, op0=op0),
                        reads=reads, writes=writes)
        return K.op("dve", lambda e: e.tensor_scalar(out=out, in0=in0, scalar1=s1, scalar2=s2, op0=op0, op1=op1),
                    reads=reads, writes=writes)

    def cp(out, in_, reads, writes, eng="dve"):
        return K.op(eng, lambda e: e.tensor_copy(out=out, in_=in_), reads=reads, writes=writes)

    def recip(out, in_, reads, writes, fast=False):
        if fast and OPT_RF:
            return K.op("dve", lambda e: e.reciprocal_approx_fast(out=out, in_=in_), reads=reads, writes=writes)
        return K.op("dve", lambda e: e.reciprocal(out=out, in_=in_), reads=reads, writes=writes)

    B_const = Buf("const")
    B_cT = Buf("cT")
    s_c3 = dsem()
    dma("sp", cT[:].rearrange("p (k r) -> p k r", r=2), cT_d, s_c3, writes=[B_cT])
    s_c = dsem()
    dma("pool", consts[:], consts_d, s_c, writes=[B_const])
    s_c2 = dsem()
    dma("sp", gvec[:], gvec_d, s_c2, writes=[B_const])
    s_c4 = dsem()
    dma("sp", bmodT[:], bmodT_d, s_c4, writes=[B_const])
    s_c5 = dsem()
    dma("sp", identf[:], consts_d[:, 0, :], s_c5, writes=[B_const])
    B_c2 = Buf("const2")
    dve(lambda e: e.memset(onesf[:], 1.0), [], [B_c2])
    dve(lambda e: e.memset(epsc[:], EPS), [], [B_c2])
    B_scT = Buf("scT")
    act(scT[:], cT[:], AF.Silu, [B_cT], [B_scT])

    hT = carve(0, [8, T], BF16)
    qa = carve(36, [4, S], BF16)
    ka = carve(52, [2, T], BF16)
    va = carve(61, [NKT, 320], BF16)
    cqn = carve(72.5, [3, S], BF16)
    ckvn = carve(84.5, [2, T], BF16)
    kr = carve(93.5, [T], BF16)
    tabB = carve(98, [2, S], F32)
    tabA = carve(114, [2, S], F32)
    xstg = [carve(72.5 + 4 * i, [D], F32) for i in range(6)]
    xnb = [carve(142 + 2 * i, [D], BF16) for i in range(2)]
    wring = [carve(146 + 2 * i, [8, 128], BF16) for i in range(8)]
    sqb = carve(162, [3, 512], BF16)
    rsb = [carve(165 + 2 * i, [512], F32) for i in range(2)]
    t1b = [carve(169 + 2 * i, [512], F32) for i in range(2)]
    t2b = [carve(173 + 2 * i, [512], F32) for i in range(2)]
    wmodst = [carve(177 + 2 * i, [8, 128], BF16) for i in range(4)] + [carve(130 + 2 * i, [8, 128], BF16) for i in range(6)]
    NMODR = 10

    B_hT = [[Buf(f"hT{g}_{k}") for k in range(8)] for g in range(5)]
    B_tabA, B_tabB = Buf("tabA"), Buf("tabB")
    B_xstg = [Buf(f"xstg{i}") for i in range(6)]
    B_xnb = [Buf(f"xnb{i}") for i in range(2)]
    B_wring = [Buf(f"wring{i}") for i in range(8)]
    S_wring = [dsem() for _ in range(8)]
    S_xstg = [dsem() for _ in range(6)]
    B_wmodst = [Buf(f"wmodst{i}") for i in range(NMODR)]
    S_wmodst = [dsem() for _ in range(NMODR)]
    B_modT = Buf("modT")
    B_opsc = Buf("opsc")

    s_t = dsem()
    s_t2 = dsem()

    def load_tables():
        dma("sp", tabA, tabA_d, s_t, writes=[B_tabA])
        dma("sp", tabB, tabB_d, s_t2, writes=[B_tabB])

    PS_mod = PS[7]
    ps_mod = ps[7]

    W_P2 = []
    for c_ in range(4):
        W_P2 += [CH_Q + c_, CH_QR + c_]
    for c_ in range(2):
        W_P2 += [CH_K + c_, CH_KR + c_]
    W_P2 += [CH_V, CH_CQ, CH_CQ + 1, CH_CQ + 2, CH_CKV, CH_CKV + 1, CH_KRR]
    W_P4 = []
    for c_ in range(8):
        W_P4 += [CH_GA + c_, CH_GB + c_]
    w_state = {"order": W_P2, "issued": 0, "base": 0}

    def w_issue(upto):
        upto = min(upto, len(w_state["order"]))
        while w_state["issued"] < upto:
            i = w_state["issued"]
            slot = (w_state["base"] + i) % 8
            dma("pool", wring[slot], win_d[w_state["order"][i]], S_wring[slot], writes=[B_wring[slot]])
            w_state["issued"] += 1

    def load_w(chunk):
        i = w_state["order"].index(chunk)
        w_issue(i + 1)
        return (w_state["base"] + i) % 8

    def w_ahead(chunk):
        w_issue(w_state["order"].index(chunk) + 8)

    def w_prefetch_p2():
        w_issue(8)

    mod_issued = [0]

    def mod_issue(upto):
        while mod_issued[0] < upto:
            j = mod_issued[0]
            wi = j % NMODR
            dma("pool", wmodst[wi], wmod_d[j], S_wmodst[wi], writes=[B_wmodst[wi]])
            mod_issued[0] += 1

    def mod_mm(j, limit):
        mod_issue(min(limit, j + NMODR))
        wi = j % NMODR
        for k in range(8):
            mm(ps_mod[:, 2 * j:2 * j + 2], wmodst[wi][:, k, :], scT[:].rearrange("p (k r) -> p k r", r=2)[:, k, :],
               k == 0, k == 7, [B_wmodst[wi], B_scT], [PS_mod])

    def mod_chunks(j0, j1, limit, mm_done=False):
        if not mm_done:
            for j in range(j0, j1):
                mod_mm(j, limit)
        for r in range(2):
            src = ps_mod[:, 2 * j0:2 * j1].rearrange("p (j r) -> p r j", r=2)[:, r, :]
            tt(modT[:, r, j0:j1], src, bmodT[:, j0:j1], ALU.add, [PS_mod, B_const], [B_modT])

    mod_issue(NMODR)

    def ln_stats(src, slot, B_src, B_stat, defer_recip=False):
        for hh in range(2):
            dve(lambda e, hh=hh: e.bn_stats(out=stat[:, slot, 6 * hh:6 * hh + 6], in_=src[:, 512 * hh:512 * hh + 512]),
                [B_src], [B_stat])
        dve(lambda e: e.bn_aggr(out=mv[:, slot, :], in_=stat[:, slot, :]), [B_stat], [B_stat])
        act(rstd[:, slot, 0:1], mv[:, slot, 1:2], AF.Sqrt, [B_stat, B_c2], [B_stat], bias=epsc[:, 0:1])
        if not defer_recip:
            recip(rstd[:, slot, 1:2], rstd[:, slot, 0:1], [B_stat], [B_stat])

    def ln_recip(slot, B_stat):
        recip(rstd[:, slot, 1:2], rstd[:, slot, 0:1], [B_stat], [B_stat])

    B_stat = [Buf(f"stat{i}") for i in range(8)]
    LA = 3
    NT1 = 18

    def tile_info(ti):
        if ti < 2:
            return 0, ti
        return 1 + (ti - 2) // 4, (ti - 2) % 4

    for step in range(NT1 + LA):
        if step < NT1:
            ti = step
            slot, sl = ti % 6, ti % 8
            src = ctx_d[ti * 128:(ti + 1) * 128, :] if ti < 2 else x_d[(ti - 2) * 128:(ti - 1) * 128, :]
            dma("sp", xstg[slot], src, S_xstg[slot], writes=[B_xstg[slot]])
            ln_stats(xstg[slot], sl, B_xstg[slot], B_stat[sl], defer_recip=True)
            if step == 5:
                load_tables()
        if 1 <= step <= NT1:
            ln_recip((step - 1) % 8, B_stat[(step - 1) % 8])
        if step < 16:
            mod_mm(step, 16)
        j = step - LA
        if j < 0:
            continue
        ti = j
        g, s_ = tile_info(ti)
        slot, sl, nb = ti % 6, ti % 8, ti % 2
        banks = [0, 1, 2, 3] if g % 2 == 0 else [4, 5, 6, 3]
        ts(xnb[nb], xstg[slot], mv[:, sl, 0:1], rstd[:, sl, 1:2], ALU.subtract, ALU.mult,
           [B_xstg[slot], B_stat[sl]], [B_xnb[nb]])
        for k in range(8):
            bk = banks[k // 2]
            pv = ps[bk][:].bitcast(BF16)
            col = (k % 2) * 512 + s_ * 128
            K.op("pe", lambda e, pv=pv, col=col, k=k, nb=nb: e.transpose(pv[:, col:col + 128], xnb[nb][:, k * 128:(k + 1) * 128], ident),
                 reads=[B_xnb[nb], B_const], writes=[PS[bk]])
        ntile = 2 if g == 0 else 4
        if s_ == ntile - 1:
            n = ntile * 128
            t0 = 0 if g == 0 else CT + (g - 1) * 512
            r = 1 if g == 0 else 0
            for k in range(8):
                bk = banks[k // 2]
                pv = ps[bk][:].bitcast(BF16)
                col = (k % 2) * 512
                act(hT[:, k, t0:t0 + n], pv[:, col:col + n], AF.Identity, [PS[bk]], [B_hT[g][k]])

    mod_chunks(0, 16, 16, mm_done=True)
    w_prefetch_p2()
    for r in range(2):
        ts(opsc[:, r, 0:8], modT[:, r, 8:16], 1.0, None, ALU.add, None, [B_modT], [B_opsc])
    for g in range(5):
        n = CT if g == 0 else 512
        t0 = 0 if g == 0 else CT + (g - 1) * 512
        r = 1 if g == 0 else 0
        for k in range(8):
            K.op("dve", lambda e, k=k, t0=t0, n=n, r=r: e.tensor_scalar(out=hT[:, k, t0:t0 + n], in0=hT[:, k, t0:t0 + n],
                                                                    scalar1=opsc[:, r, k:k + 1], scalar2=modT[:, r, k:k + 1],
                                                                    op0=ALU.mult, op1=ALU.add),
                 reads=[B_opsc, B_modT], writes=[B_hT[g][k]])

    B_qa, B_ka, B_va, B_cqn, B_ckvn, B_kr = Buf("qa"), Buf("ka"), Buf("va"), Buf("cqn"), Buf("ckvn"), Buf("kr")
    B_sq = Buf("sq")
    B_rs = [Buf("rs0"), Buf("rs1")]
    B_t1 = [Buf("t10"), Buf("t11")]
    B_t2 = [Buf("t20"), Buf("t21")]
    groups = [(0, 0, CT)] + [(g, CT + (g - 1) * 512, 512) for g in range(1, 5)]
    pcount = [0]

    def nextps(n=1):
        r = []
        for _ in range(n):
            r.append(pcount[0] % 7)
            pcount[0] += 1
        return r

    def proj_fm(wi, g, t0, n, bank, m0=0, m1=128, prow0=0):
        for k in range(8):
            mm(ps[bank][prow0:prow0 + (m1 - m0), 0:n], wring[wi][:, k, m0:m1], hT[:, k, t0:t0 + n], k == 0, k == 7,
               [B_wring[wi], B_hT[g][k]], [PS[bank]])

    it = [0]

    def qk_unit(ch_p, ch_r, gcol, dst, dst_idx, B_dst, latent_only, nheaddim=64):
        wi_p = load_w(ch_p)
        wi_r = load_w(ch_r)
        w_ahead(ch_p)
        for (g, t0, n) in groups:
            if latent_only and g == 0:
                continue
            i2 = it[0] % 2
            it[0] += 1
            bp, br, bs = nextps(3)
            proj_fm(wi_p, g, t0, n, bp)
            act(sqb[:, 0, 0:n], ps[bp][:, 0:n], AF.Square, [PS[bp]], [B_sq])
            if g != 0:
                proj_fm(wi_r, g, t0, n, br)
            mm(ps[bs][:, 0:n], blockones, sqb[:, 0, 0:n], True, True, [B_sq, B_const], [PS[bs]])
            if OPT_LN:
                act(rsb[i2][:, 0:n], ps[bs][:, 0:n], AF.Ln, [PS[bs], B_c2], [B_rs[i2]], scale=1.0 / nheaddim, bias=epsc[:, 0:1])
                act(rsb[i2][:, 0:n], rsb[i2][:, 0:n], AF.Exp, [B_rs[i2]], [B_rs[i2]], scale=-0.5)
            else:
                act(rsb[i2][:, 0:n], ps[bs][:, 0:n], AF.Sqrt, [PS[bs], B_c2], [B_rs[i2]], scale=1.0 / nheaddim, bias=epsc[:, 0:1])
                recip(rsb[i2][:, 0:n], rsb[i2][:, 0:n], [B_rs[i2]], [B_rs[i2]], fast=True)
            tq0 = t0 - CT if not latent_only else t0 - CT
            if g == 0:
                stt(dst[:, dst_idx, t0:t0 + n], ps[bp][:, 0:n], gvec[:, gcol:gcol + 1], rsb[i2][:, 0:n], ALU.mult, ALU.mult,
                    [PS[bp], B_rs[i2], B_const], [B_dst])
                continue
            stt(t1b[i2][:, 0:n], ps[bp][:, 0:n], gvec[:, gcol:gcol + 1], tabA[:, 0, tq0:tq0 + n], ALU.mult, ALU.mult,
                [PS[bp], B_tabA, B_const], [B_t1[i2]])
            stt(t2b[i2][:, 0:n], ps[br][:, 0:n], gvec[:, gcol + 1:gcol + 2], tabA[:, 1, tq0:tq0 + n], ALU.mult, ALU.mult,
                [PS[br], B_tabA, B_const], [B_t2[i2]])
            tt(t1b[i2][:, 0:n], t1b[i2][:, 0:n], t2b[i2][:, 0:n], ALU.add, [B_t1[i2], B_t2[i2]], [B_t1[i2]], eng="pool")
            dcol0 = t0 - CT if latent_only else t0
            tt(dst[:, dst_idx, dcol0:dcol0 + n], t1b[i2][:, 0:n], rsb[i2][:, 0:n], ALU.mult, [B_t1[i2], B_rs[i2]], [B_dst], eng="pool")

    mod_next = [16]

    def mod_piece(nchunks=4):
        j0 = mod_next[0]
        j1 = min(48, j0 + nchunks)
        if j1 > j0:
            mod_chunks(j0, j1, 48)
            mod_next[0] = j1

    for c in range(4):
        qk_unit(CH_Q + c, CH_QR + c, 0, qa, c, B_qa, True)
        mod_piece()
    for kk in range(2):
        qk_unit(CH_K + kk, CH_KR + kk, 2, ka, kk, B_ka, False)
        mod_piece()

    for cc in (0, 128, 256):
        dve(lambda e, cc=cc: e.memset(va[:, :, cc:cc + 64], 1.0), [], [B_va])
    wi_v = load_w(CH_V)
    w_ahead(CH_V)
    for kt0 in range(0, NKT, 4):
        nk = min(4, NKT - kt0)
        (bv,) = nextps(1)
        for s_ in range(nk):
            kt = kt0 + s_
            g = 0 if kt < 2 else 1 + (kt - 2) // 4
            for k in range(8):
                mm(ps[bv][:, s_ * 128:(s_ + 1) * 128], hT[:, k, kt * 128:(kt + 1) * 128], wring[wi_v][:, k, :], k == 0, k == 7,
                   [B_wring[wi_v], B_hT[g][k]], [PS[bv]])
        src = ps[bv][:, 0:nk * 128].rearrange("p (s c) -> p s c", c=128)
        cp(va[:, kt0:kt0 + nk, 64:128], src[:, :, 0:64], [PS[bv]], [B_va])
        cp(va[:, kt0:kt0 + nk, 192:256], src[:, :, 64:128], [PS[bv]], [B_va])

    def lowrank_unit(ch0, nch, gcol0, dst, B_dst, latent_only, rank):
        wis = [load_w(ch0 + j) for j in range(nch)]
        w_ahead(ch0)
        for (g, t0, n) in groups:
            if latent_only and g == 0:
                continue
            i2 = it[0] % 2
            it[0] += 1
            banks = nextps(nch + 1)
            for j in range(nch):
                proj_fm(wis[j], g, t0, n, banks[j])
                act(sqb[:, j, 0:n], ps[banks[j]][:, 0:n], AF.Square, [PS[banks[j]]], [B_sq])
            bs = banks[nch]
            for j in range(nch):
                mm(ps[bs][:, 0:n], ones128, sqb[:, j, 0:n], j == 0, j == nch - 1, [B_sq, B_const], [PS[bs]])
            if OPT_LN:
                act(rsb[i2][:, 0:n], ps[bs][:, 0:n], AF.Ln, [PS[bs], B_c2], [B_rs[i2]], scale=1.0 / rank, bias=epsc[:, 0:1])
                act(rsb[i2][:, 0:n], rsb[i2][:, 0:n], AF.Exp, [B_rs[i2]], [B_rs[i2]], scale=-0.5)
            else:
                act(rsb[i2][:, 0:n], ps[bs][:, 0:n], AF.Sqrt, [PS[bs], B_c2], [B_rs[i2]], scale=1.0 / rank, bias=epsc[:, 0:1])
                recip(rsb[i2][:, 0:n], rsb[i2][:, 0:n], [B_rs[i2]], [B_rs[i2]], fast=True)
            dcol0 = t0 - CT if latent_only else t0
            for j in range(nch):
                stt(dst[:, j, dcol0:dcol0 + n], ps[banks[j]][:, 0:n], gvec[:, gcol0 + j:gcol0 + j + 1], rsb[i2][:, 0:n],
                    ALU.mult, ALU.mult, [PS[banks[j]], B_rs[i2], B_const], [B_dst] + B_xstg)

    mod_piece()
    lowrank_unit(CH_CQ, 3, 4, cqn, B_cqn, True, 384)
    mod_piece()
    lowrank_unit(CH_CKV, 2, 7, ckvn, B_ckvn, False, 256)

    wi_kr = load_w(CH_KRR)
    for (g, t0, n) in groups:
        i2 = it[0] % 2
        it[0] += 1
        bp, br = nextps(2)
        proj_fm(wi_kr, g, t0, n, bp, 0, 32, 64)
        if g == 0:
            cp(kr[64:96, t0:t0 + n], ps[bp][64:96, 0:n], [PS[bp]], [B_kr] + B_xstg)
            continue
        proj_fm(wi_kr, g, t0, n, br, 32, 64, 64)
        tq0 = t0 - CT
        tt(t1b[i2][64:96, 0:n], ps[bp][64:96, 0:n], tabB[64:96, 0, tq0:tq0 + n], ALU.mult, [PS[bp], B_tabB], [B_t1[i2]])
        tt(t2b[i2][64:96, 0:n], ps[br][64:96, 0:n], tabB[64:96, 1, tq0:tq0 + n], ALU.mult, [PS[br], B_tabB], [B_t2[i2]])
        tt(kr[64:96, t0:t0 + n], t1b[i2][64:96, 0:n], t2b[i2][64:96, 0:n], ALU.add, [B_t1[i2], B_t2[i2]], [B_kr] + B_xstg)

    mod_piece(48)
    ts(opsc[:, 0, 8:16], modT[:, 0, 32:40], 1.0, None, ALU.add, None, [B_modT], [B_opsc])

    out_sems = [dsem() for _ in range(4)]
    B_dbg = Buf("dbg")
    n_out = [0]

    def store(dst, src, reads):
        s = out_sems[n_out[0] % 4]
        n_out[0] += 1
        return dma("sp", dst, src, s, reads=reads, writes=[B_dbg])

    if DEBUG:
        store(dbg["d_hT"], hT.rearrange("p a b -> p (a b)"), [b for gg in B_hT for b in gg])
        store(dbg["d_qa"], qa.rearrange("p a b -> p (a b)"), [B_qa])
        store(dbg["d_ka"], ka.rearrange("p a b -> p (a b)"), [B_ka])
        store(dbg["d_va"], va.rearrange("p a b -> p (a b)"), [B_va])
        store(dbg["d_cqn"], cqn.rearrange("p a b -> p (a b)"), [B_cqn])
        store(dbg["d_ckvn"], ckvn.rearrange("p a b -> p (a b)"), [B_ckvn])
        store(dbg["d_kr"], kr, [B_kr])
        store(dbg["d_modT"], modT[:].rearrange("p a b -> p (a b)"), [B_modT])

    K.barrier(extra=[x.last for x in out_sems if x.last is not None])
    release(S_xstg + S_wmodst + [s_c, s_c2, s_c3, s_c4, s_c5, s_t, s_t2])
    oa = carve(114, [4, S], BF16)
    PT = [carve(130 + 2 * i, [1024], BF16) for i in range(3)]
    recb = [carve(136 + 2 * i, [512], F32) for i in range(2)]
    wuq = carve(140, [3, 1024], BF16)
    wukv = carve(146, [2, 1024], BF16)
    qb = [carve(150 + 4 * i, [S], BF16) for i in range(2)]
    kb = [carve(158 + 4.5 * i, [T], BF16) for i in range(2)]
    vb = [carve(167 + 6.75 * i, [NKT, 192], BF16) for i in range(2)]
    ob = carve(36, [4, S], BF16)
    r1b = [carve(181 + 2 * i, [512], F32) for i in range(2)]
    r2b = [carve(185 + 2 * i, [512], F32) for i in range(2)]
    B_r1 = [Buf("r10"), Buf("r11")]
    B_r2 = [Buf("r20"), Buf("r21")]
    B_oa, B_ob = Buf("oa"), Buf("ob")
    B_PT = [Buf(f"PT{i}") for i in range(3)]
    B_rec = [Buf("rec0"), Buf("rec1")]
    B_wuq, B_wukv = Buf("wuq"), Buf("wukv")
    B_qb = [Buf("qb0"), Buf("qb1")]
    B_kb = [Buf("kb0"), Buf("kb1")]
    B_vb = [Buf("vb0"), Buf("vb1")]
    s_wuq = [dsem() for _ in range(5)]
    for k in range(3):
        dma("pool", wuq[:, k, :], wuq_d[:, k, :], s_wuq[k], writes=[B_wuq])
    for k in range(2):
        dma("pool", wukv[:, k, :], wukv_d[:, k, :], s_wuq[3 + k], writes=[B_wukv])

    class Step:
        __slots__ = ("pre", "S", "E", "PV", "post")

        def __init__(self):
            self.pre = self.S = self.E = self.PV = self.post = None

    def run_pipeline(steps):
        n = len(steps)
        for i in range(n + 2):
            if i < n:
                st_ = steps[i]
                if st_.pre is not None:
                    st_.pre()
                st_.S()
                st_.E()
            if i >= 2:
                st_ = steps[i - 2]
                st_.PV()
                if st_.post is not None:
                    st_.post()

    def finalize(obank, par, dst, B_dst, c, qt, ri, on_act=False):
        tok = slice(qt * 512, (qt + 1) * 512)
        lo, hi = (slice(0, 64), slice(64, 128)) if par == 0 else (slice(64, 128), slice(0, 64))
        if on_act:
            act(recb[ri][lo, :], ps[obank][hi, :], AF.Ln, [PS[obank]], [B_rec[ri]])
            act(recb[ri][lo, :], recb[ri][lo, :], AF.Exp, [B_rec[ri]], [B_rec[ri]], scale=-1.0)
        else:
            recip(recb[ri][lo, :], ps[obank][hi, :], [PS[obank]], [B_rec[ri]], fast=True)
        wr = [B_dst, B_qa] if on_act else [B_dst]
        tt(dst[lo, c, tok], ps[obank][lo, :], recb[ri][lo, :], ALU.mult, [PS[obank], B_rec[ri]], wr)

    steps = []
    idx = 0
    for c in range(4):
        kvh = c // 2
        for qt in range(4):
            oi = c * 4 + qt
            ob0, ob1 = (4, 5) if oi % 2 == 0 else (6, 7)
            for kt in range(NKT):
                st_ = Step()
                sbk = idx % 2
                pi = idx % 3
                idx += 1
                b0, b1 = 2 * sbk, 2 * sbk + 1
                qs = slice(qt * 512, (qt + 1) * 512)
                ks = slice(kt * 128, (kt + 1) * 128)

                def S_(b0=b0, b1=b1, c=c, kvh=kvh, qs=qs, ks=ks):
                    mm(ps[b0][:, :], ka[0:64, kvh, ks], qa[0:64, c, qs], True, True, [B_qa, B_ka], [PS[b0]])
                    mm(ps[b1][:, :], ka[64:128, kvh, ks], qa[64:128, c, qs], True, True, [B_qa, B_ka], [PS[b1]])

                def E_(b0=b0, b1=b1, pi=pi, sbk=sbk):
                    act(PT[pi][:, 0:1024], psw[sbk][:, :], AF.Exp, [PS[b0], PS[b1]], [B_PT[pi]], scale=A_SCALE)

                def PV_(ob0=ob0, ob1=ob1, pi=pi, kt=kt, kvh=kvh):
                    mm(ps[ob0][:, :], va[:, kt, 64 + 128 * kvh:192 + 128 * kvh], PT[pi][:, 0:512], kt == 0, kt == NKT - 1,
                       [B_PT[pi], B_va], [PS[ob0]])
                    mm(ps[ob1][:, :], va[:, kt, 128 * kvh:128 + 128 * kvh], PT[pi][:, 512:1024], kt == 0, kt == NKT - 1,
                       [B_PT[pi], B_va], [PS[ob1]])

                st_.S, st_.E, st_.PV = S_, E_, PV_
                if kt == NKT - 1:
                    def post_(ob0=ob0, ob1=ob1, c=c, qt=qt, oi=oi):
                        finalize(ob0, 0, oa, B_oa, c, qt, oi % 2)
                        finalize(ob1, 1, oa, B_oa, c, qt, oi % 2)
                    st_.post = post_
                steps.append(st_)
    steps_A = steps

    jit_i = [0]

    def jitps():
        b = 6 + (jit_i[0] % 2)
        jit_i[0] += 1
        return b

    for bi in range(2):
        dve(lambda e, bi=bi: e.memset(vb[bi][:, :, 0:64], 1.0), [], [B_vb[bi]])
        dve(lambda e, bi=bi: e.memset(vb[bi][:, :, 128:192], 1.0), [], [B_vb[bi]])
        cp(kb[bi][64:96, :], kr[64:96, :], [B_kr], [B_kb[bi]])

    def jit_tasks(h):
        bi = h % 2
        tasks = []
        for (g, t0, n) in groups:
            def t_k(g=g, t0=t0, n=n):
                bk = jitps()
                for k in range(2):
                    mm(ps[bk][0:64, 0:n], wukv[:, k, h * 128:h * 128 + 64], ckvn[:, k, t0:t0 + n], k == 0, k == 1,
                       [B_wukv, B_ckvn], [PS[bk]])
                cp(kb[bi][0:64, t0:t0 + n], ps[bk][0:64, 0:n], [PS[bk]], [B_kb[bi]])
            tasks.append(t_k)
        for kt0 in range(0, NKT, 3):
            def t_v(kt0=kt0):
                nk = min(3, NKT - kt0)
                bk = jitps()
                for s_ in range(nk):
                    kt = kt0 + s_
                    for k in range(2):
                        mm(ps[bk][:, s_ * 64:(s_ + 1) * 64], ckvn[:, k, kt * 128:(kt + 1) * 128],
                           wukv[:, k, h * 128 + 64:h * 128 + 128], k == 0, k == 1, [B_wukv, B_ckvn], [PS[bk]])
                cp(vb[bi][:, kt0:kt0 + nk, 64:128], ps[bk][:, 0:nk * 64].rearrange("p (s c) -> p s c", c=64), [PS[bk]], [B_vb[bi]])
            tasks.append(t_v)
        for qt in range(4):
            def t_q1(qt=qt):
                bq = jitps()
                tok = slice(qt * 512, (qt + 1) * 512)
                i2 = qt % 2
                for k in range(3):
                    mm(ps[bq][0:96, :], wuq[:, k, h * 96:(h + 1) * 96], cqn[:, k, tok], k == 0, k == 2, [B_wuq, B_cqn], [PS[bq]])
                cp(qb[bi][0:64, tok], ps[bq][0:64, :], [PS[bq]], [B_qb[bi]])
                tt(r1b[i2][64:96, :], ps[bq][64:96, :], tabB[64:96, 0, tok], ALU.mult, [PS[bq], B_tabB], [B_r1[i2]])

            def t_q2(qt=qt):
                br = jitps()
                tok = slice(qt * 512, (qt + 1) * 512)
                i2 = qt % 2
                for k in range(3):
                    mm(ps[br][64:96, :], wuq[:, k, 768 + h * 32:768 + (h + 1) * 32], cqn[:, k, tok], k == 0, k == 2,
                       [B_wuq, B_cqn], [PS[br]])
                tt(r2b[i2][64:96, :], ps[br][64:96, :], tabB[64:96, 1, tok], ALU.mult, [PS[br], B_tabB], [B_r2[i2]])
                tt(qb[bi][64:96, tok], r1b[i2][64:96, :], r2b[i2][64:96, :], ALU.add, [B_r1[i2], B_r2[i2]], [B_qb[bi]], eng="pool")
            tasks.append(t_q1)
            tasks.append(t_q2)
        return tasks

    def jit_head(h):
        for t_ in jit_tasks(h):
            t_()

    pend0 = jit_tasks(1) + jit_tasks(0)
    a_slots = [st_ for (st_, (c_, qt_, kt_)) in zip(steps_A, [(c_, qt_, kt_) for c_ in range(4) for qt_ in range(4) for kt_ in range(NKT)])
               if c_ in (2, 3) and qt_ in (0, 2) and kt_ >= 2]
    assert len(a_slots) >= len(pend0)
    for st_, t_ in zip(a_slots, pend0):
        st_.pre = t_
    steps = []
    for h in range(8):
        bi = h % 2
        c, par = h // 2, h % 2
        for qt in range(4):
            oi = h * 4 + qt
            obk = 4 + (oi % 2)
            for kp in range(NKT // 2):
                st_ = Step()
                sbk = idx % 2
                pi = idx % 3
                idx += 1
                b0, b1 = 2 * sbk, 2 * sbk + 1
                qs = slice(qt * 512, (qt + 1) * 512)

                def S_(b0=b0, b1=b1, bi=bi, qs=qs, kp=kp):
                    for j, bj in enumerate((b0, b1)):
                        kt = 2 * kp + j
                        mm(ps[bj][:, :], kb[bi][0:96, kt * 128:(kt + 1) * 128], qb[bi][0:96, qs], True, True,
                           [B_qb[bi], B_kb[bi]], [PS[bj]])

                def E_(b0=b0, b1=b1, pi=pi, sbk=sbk):
                    act(PT[pi][:, 0:1024], psw[sbk][:, :], AF.Exp, [PS[b0], PS[b1]], [B_PT[pi]], scale=B_SCALE)

                def PV_(obk=obk, pi=pi, kp=kp, bi=bi, par=par):
                    for j in range(2):
                        kt = 2 * kp + j
                        lhs = vb[bi][:, kt, 64:192] if par == 0 else vb[bi][:, kt, 0:128]
                        mm(ps[obk][:, :], lhs, PT[pi][:, j * 512:(j + 1) * 512], kt == 0, kt == NKT - 1,
                           [B_PT[pi], B_vb[bi]], [PS[obk]])

                st_.S, st_.E, st_.PV = S_, E_, PV_
                if kp == NKT // 2 - 1:
                    def post_(obk=obk, par=par, c=c, qt=qt, oi=oi):
                        finalize(obk, par, ob, B_ob, c, qt, oi % 2, on_act=True)
                    st_.post = post_
                if 1 <= h and h + 1 < 8:
                    li = qt * (NKT // 2) + kp
                    if li == 0:
                        pending = jit_tasks(h + 1)
                    if li >= 2 and (li - 2) < len(pending):
                        st_.pre = pending[li - 2]
                steps.append(st_)
    run_pipeline(steps_A + steps)
    if DEBUG:
        store(dbg["d_oa"], oa.rearrange("p a b -> p (a b)"), [B_oa])
        store(dbg["d_ob"], ob.rearrange("p a b -> p (a b)"), [B_ob])
        store(dbg["d_qb"], qb[1], [B_qb[1]])
        store(dbg["d_kb"], kb[1], [B_kb[1]])
        store(dbg["d_vb"], vb[1].rearrange("p a b -> p (a b)"), [B_vb[1]])
    K.barrier(extra=[x.last for x in out_sems if x.last is not None])

    release(s_wuq)
    mT = carve(52, [8, S], BF16)
    sab = [carve(84 + 2 * i, [512], F32) for i in range(4)]
    mab = [carve(92 + 2 * i, [512], F32) for i in range(4)]
    pring = [carve(100 + 1 * i, [4, 128], BF16) for i in range(4)]
    wout = carve(164, [8, 1024], BF16)
    B_mT = Buf("mT")
    gtb = carve(130, [D], F32)
    B_gtb = Buf("gtb")
    B_bct = Buf("bctmp")

    def bcast_mod(dst, B_dst_, j0):
        for half in range(2):
            b0, = nextps(1)
            for jj in range(4):
                j = j0 + half * 4 + jj
                i2 = it[0] % 2
                it[0] += 1
                ts(bctmp[:, i2, :], onesf[:], modT[:, 0, j:j + 1], None, ALU.mult, None, [B_c2, B_modT], [B_bct])
                mm(ps[b0][:, jj * 128:(jj + 1) * 128], bctmp[:, i2, :], identf[:], True, True, [B_bct, B_const], [PS[b0]])
            cp(dst[:, half * 512:(half + 1) * 512], ps[b0][:, :], [PS[b0]], [B_dst_])

    B_sab = [Buf(f"sab{i}") for i in range(4)]
    B_mab = [Buf(f"mab{i}") for i in range(4)]
    B_pring = [Buf(f"pring{i}") for i in range(4)]
    S_pring = [dsem() for _ in range(4)]
    B_wout = Buf("wout")
    s_wout = [dsem() for _ in range(8)]
    pr_next = [0]
    w_state.update(order=W_P4, issued=0, base=0)
    for c in range(8):
        wga = load_w(CH_GA + c)
        wgb = load_w(CH_GB + c)
        w_ahead(CH_GA + c)
        pa_i = pr_next[0] % 4
        pr_next[0] += 1
        dma("pool", pring[pa_i], wpa_d[c], S_pring[pa_i], writes=[B_pring[pa_i]])
        pb_i = pr_next[0] % 4
        pr_next[0] += 1
        dma("pool", pring[pb_i], wpb_d[c], S_pring[pb_i], writes=[B_pring[pb_i]])
        if c == 0:
            for k in range(8):
                dma("pool", wout[:, k, :], wout_d[:, k, :], s_wout[k], writes=[B_wout])
        if c == 1:
            bcast_mod(gtb, B_gtb, 16)
            for k in range(8):
                tt(wout[:, k, :], wout[:, k, :], gtb, ALU.mult, [B_wout, B_gtb], [B_wout], eng="pool")
        for qt in range(4):
            g = qt + 1
            t0 = CT + qt * 512
            tok = slice(qt * 512, (qt + 1) * 512)
            i2 = it[0] % 2
            it[0] += 1
            bga, bgb, bpa, bpb = nextps(4)
            proj_fm(wga, g, t0, 512, bga)
            proj_fm(wgb, g, t0, 512, bgb)
            for k in range(4):
                mm(ps[bpa][:, :], pring[pa_i][:, k, :], oa[:, k, tok], k == 0, k == 3, [B_pring[pa_i], B_oa], [PS[bpa]])
            for k in range(4):
                mm(ps[bpb][:, :], pring[pb_i][:, k, :], ob[:, k, tok], k == 0, k == 3, [B_pring[pb_i], B_ob], [PS[bpb]])
            act(sab[i2][:, :], ps[bga][:, :], AF.Sigmoid, [PS[bga]], [B_sab[i2]])
            act(sab[2 + i2][:, :], ps[bgb][:, :], AF.Sigmoid, [PS[bgb]], [B_sab[2 + i2]])
            tt(mab[i2][:, :], sab[i2][:, :], ps[bpa][:, :], ALU.mult, [B_sab[i2], PS[bpa]], [B_mab[i2]])
            tt(mab[2 + i2][:, :], sab[2 + i2][:, :], ps[bpb][:, :], ALU.mult, [B_sab[2 + i2], PS[bpb]], [B_mab[2 + i2]])
            tt(mT[:, c, tok], mab[i2][:, :], mab[2 + i2][:, :], ALU.add, [B_mab[i2], B_mab[2 + i2]], [B_mT])
    if DEBUG:
        store(dbg["d_mT"], mT.rearrange("p a b -> p (a b)"), [B_mT])
    K.barrier(extra=[x.last for x in out_sems if x.last is not None])

    release(S_pring + S_wring)
    x1 = carve(100, [16, D], F32)
    bc1 = [carve(0 + 4 * i, [D], F32) for i in range(3)]
    xs2 = [carve(12 + 4 * i, [D], F32) for i in range(2)]
    zb = [carve(20 + 4 * i, [D], F32) for i in range(2)]
    B_bc1 = [Buf(f"bc1_{i}") for i in range(3)]
    B_xs2 = [Buf("xs2_0"), Buf("xs2_1")]
    S_xs2 = [dsem(), dsem()]
    B_zb = [Buf("zb0"), Buf("zb1")]
    B_x1 = [Buf(f"x1_{i}") for i in range(16)]
    s_ln = [dsem() for _ in range(4)]

    dma("sp", bc1[1], lnrows_d[0:1, :].partition_broadcast(128), s_ln[0], writes=[B_bc1[1]])
    dma("sp", bc1[2], lnrows_d[1:2, :].partition_broadcast(128), s_ln[1], writes=[B_bc1[2]])

    def post_norm(y_banks, xsrc, B_xsrc, gt_bc, B_gt, g_bc, B_g, b_bc, B_b, dst, B_dst_, zi, sl):
        z = zb[zi]
        for hh in range(2):
            hs = slice(hh * 512, (hh + 1) * 512)
            stt(z[:, hs], xsrc[:, hs], ALPHA, ps[y_banks[hh]][:, :], ALU.mult, ALU.add,
                [B_xsrc, PS[y_banks[hh]]], [B_zb[zi]])
        ln_stats(z, sl, B_zb[zi], B_stat[sl])
        ts(z, z, mv[:, sl, 0:1], rstd[:, sl, 1:2], ALU.subtract, ALU.mult, [B_zb[zi], B_stat[sl]], [B_zb[zi]])
        tt(z, z, g_bc, ALU.mult, [B_zb[zi], B_g], [B_zb[zi]], eng="pool")
        tt(dst, z, b_bc, ALU.add, [B_zb[zi], B_b], [B_dst_], eng="pool")

    for st in range(16):
        xi = st % 2
        dma("sp", xs2[xi], x_d[st * 128:(st + 1) * 128, :], S_xs2[xi], writes=[B_xs2[xi]])
        yb = nextps(2)
        for hh in range(2):
            for k in range(8):
                mm(ps[yb[hh]][:, :], mT[:, k, st * 128:(st + 1) * 128], wout[:, k, hh * 512:(hh + 1) * 512], k == 0, k == 7,
                   [B_mT, B_wout], [PS[yb[hh]]])
        post_norm(yb, xs2[xi], B_xs2[xi], bc1[0], B_bc1[0], bc1[1], B_bc1[1], bc1[2], B_bc1[2], x1[:, st, :], B_x1[st], st % 2, st % 4)
    if DEBUG:
        store(dbg["d_x1"], x1.rearrange("p a b -> p (a b)"), B_x1)
    K.barrier(extra=[x.last for x in out_sems if x.last is not None])

    release(s_wout + S_xs2)
    wdown = carve(0, [NF, 1024], BF16)
    actT = carve(44, [NF, 512], BF16)
    h2T = [carve(66 + 8 * i, [8, 512], BF16) for i in range(2)]
    upring = [carve(168, [8, 256], BF16), carve(180, [8, 256], BF16), carve(184, [8, 256], BF16)]
    bc2 = [carve(82 + 4 * i, [D], F32) for i in range(3)]
    xn2 = [carve(94 + 2 * i, [D], BF16) for i in range(2)]
    sa2 = [carve(164 + 2 * i, [512], F32) for i in range(2)]
    zb2 = [carve(172 + 4 * i, [D], F32) for i in range(2)]
    NUP = 3
    B_wdown = [Buf(f"wdown{j}") for j in range(NF)]
    s_wdown = [dsem() for _ in range(4)]
    B_actT = Buf("actT")
    B_h2T = [Buf("h2T0"), Buf("h2T1")]
    B_upring = [Buf(f"up{i}") for i in range(NUP)]
    S_upring = [dsem() for _ in range(NUP)]
    B_bc2 = [Buf(f"bc2_{i}") for i in range(3)]
    B_xn2 = [Buf("xn2_0"), Buf("xn2_1")]
    B_sa2 = [Buf("sa2_0"), Buf("sa2_1")]
    B_zb2 = [Buf(f"zb2_{i}") for i in range(2)]

    bcast_mod(bc2[0], B_bc2[0], 40)
    dma("sp", bc2[1], lnrows_d[2:3, :].partition_broadcast(128), s_ln[2], writes=[B_bc2[1]])
    dma("sp", bc2[2], lnrows_d[3:4, :].partition_broadcast(128), s_ln[3], writes=[B_bc2[2]])

    up_next = [0]
    wdown_loaded = [False]
    oz = [0]

    def ln2_dve(tg):
        pass

    def ln2_group(tg, banks):
        hb = tg % 2
        for s_ in range(4):
            st = tg * 4 + s_
            sl = 4 + st % 4
            ln_stats(x1[:, st, :], sl, B_x1[st], B_stat[sl], defer_recip=True)
        for s_ in range(4):
            st = tg * 4 + s_
            sl = 4 + st % 4
            nb = st % 2
            ln_recip(sl, B_stat[sl])
            ts(xn2[nb], x1[:, st, :], mv[:, sl, 0:1], rstd[:, sl, 1:2], ALU.subtract, ALU.mult, [B_x1[st], B_stat[sl]], [B_xn2[nb]])
            for k in range(8):
                bk = banks[k // 2]
                pv = ps[bk][:].bitcast(BF16)
                col = (k % 2) * 512 + s_ * 128
                K.op("pe", lambda e, pv=pv, col=col, k=k, nb=nb: e.transpose(pv[:, col:col + 128], xn2[nb][:, k * 128:(k + 1) * 128], ident),
                     reads=[B_xn2[nb], B_const], writes=[PS[bk]])
        for k in range(8):
            bk = banks[k // 2]
            pv = ps[bk][:].bitcast(BF16)
            col = (k % 2) * 512
            act(h2T[hb][:, k, :], pv[:, col:col + 512], AF.Identity, [PS[bk], B_opsc, B_modT], [B_h2T[hb]],
                scale=opsc[:, 0, 8 + k:9 + k], bias=modT[:, 0, 24 + k:25 + k])

    up_issued = [0]
    NUPQ = 4 * NF

    def up_issue(upto):
        upto = min(upto, NUPQ)
        while up_issued[0] < upto:
            q = up_issued[0]
            ui = q % NUP
            dma("pool", upring[ui], wup_d[q % NF], S_upring[ui], writes=[B_upring[ui]])
            up_issued[0] += 1
            if not wdown_loaded[0] and q == 1:
                for jj in range(NF):
                    dma("pool", wdown[:, jj, :], wdown_d[:, jj, :], s_wdown[jj % 4], writes=[B_wdown[jj]])
                wdown_loaded[0] = True

    def up_group(tg):
        hb = tg % 2
        for j in range(NF):
            q = tg * NF + j
            up_issue(q + NUP)
            ui = q % NUP
            ba, bu = (4, 5) if j % 2 == 0 else (6, 7)
            for k in range(8):
                mm(ps[ba][:, :], upring[ui][:, k, 0:128], h2T[hb][:, k, :], k == 0, k == 7, [B_upring[ui], B_h2T[hb]], [PS[ba]])
            for k in range(8):
                mm(ps[bu][:, :], upring[ui][:, k, 128:256], h2T[hb][:, k, :], k == 0, k == 7, [B_upring[ui], B_h2T[hb]], [PS[bu]])
            i2 = j % 2
            act(sa2[i2][:, :], ps[ba][:, :], AF.Silu, [PS[ba]], [B_sa2[i2]])
            tt(actT[:, j, :], sa2[i2][:, :], ps[bu][:, :], ALU.mult, [B_sa2[i2], PS[bu]], [B_actT])
        up_issue((tg + 1) * NF + NUP)

    def down_sub(tg, s_):
        st = tg * 4 + s_
        yb = (0, 1) if s_ % 2 == 0 else (2, 3)
        for hh in range(2):
            for j in range(NF):
                mm(ps[yb[hh]][:, :], actT[:, j, s_ * 128:(s_ + 1) * 128], wdown[:, j, hh * 512:(hh + 1) * 512], j == 0, j == NF - 1,
                   [B_actT, B_wdown[j]], [PS[yb[hh]]])
        return yb

    def post2(tg, s_, yb):
        st = tg * 4 + s_
        zi = oz[0] % 2
        oz[0] += 1
        z = zb2[zi]
        sl = st % 4
        for hh in range(2):
            tt(z[:, hh * 512:(hh + 1) * 512], ps[yb[hh]][:, :], bc2[0][:, hh * 512:(hh + 1) * 512], ALU.mult,
               [PS[yb[hh]], B_bc2[0]], [B_zb2[zi]])
        stt(z, x1[:, st, :], ALPHA, z, ALU.mult, ALU.add, [B_x1[st], B_zb2[zi]], [B_zb2[zi]])
        ln_stats(z, sl, B_zb2[zi], B_stat[sl])
        ts(z, z, mv[:, sl, 0:1], rstd[:, sl, 1:2], ALU.subtract, ALU.mult, [B_zb2[zi], B_stat[sl]], [B_zb2[zi]])
        tt(z, z, bc2[1], ALU.mult, [B_zb2[zi], B_bc2[1]], [B_zb2[zi]])
        tt(z, z, bc2[2], ALU.add, [B_zb2[zi], B_bc2[2]], [B_zb2[zi]])
        store(out_d[st * 128:(st + 1) * 128, :], z, [B_zb2[zi]])

    ln2_group(0, [0, 1, 2, 3])
    for tg in range(4):
        up_group(tg)
        for s_ in range(4):
            yb = down_sub(tg, s_)
            if s_ == 0 and tg + 1 < 4:
                ln2_group(tg + 1, [4, 5, 6, 7])
            post2(tg, s_, yb)

    fin = Op("sp", None, None)
    for s in out_sems:
        if s.last is not None:
            fin.deps.append(s.last)
    K.ops["sp"].append(fin)

    with nc.Block() as block:
        def run(ename):
            def f(e):
                _emit_one(K, ename, e, sems)
            return f

        block.tensor(run("pe"))
        block.scalar(run("act"))
        block.vector(run("dve"))
        block.gpsimd(run("pool"))
        block.sync(run("sp"))
    es.close()
    return nc


def _prepare_vals(K, sems):
    for e in K.ENGS:
        c = 0
        for o in K.ops[e]:
            if o.dma is None and o.sig and o.fn is not None:
                c += 1
                o.sem = sems[e]
                o.val = c


def _emit_one(K, ename, eng, sems):
    if not getattr(K, "_prepared", False):
        _prepare_vals(K, sems)
        K._prepared = True
    waited = {}
    for o in K.ops[ename]:
        for d in o.deps:
            key = id(d.sem)
            if waited.get(key, 0) < d.val:
                eng.wait_ge(d.sem, d.val)
                waited[key] = d.val
        if o.fn is None:
            continue
        inst = o.fn(eng)
        if o.sig:
            inst.then_inc(o.sem, 16 if o.dma is not None else 1)


def _rope_tables():
    theta = np.float32(10000.0)
    rows = np.repeat(np.arange(32), 64).astype(np.float32)
    cols = np.tile(np.arange(64), 32).astype(np.float32)

    def tab(dim):
        d2 = dim // 2
        half = d2 // 2
        fr = (theta ** (-np.arange(half, dtype=np.float32) / np.float32(half))).astype(np.float32)
        cos = np.zeros((dim, S), np.float32)
        sin = np.zeros((dim, S), np.float32)
        for d in range(dim):
            pos = rows if d < d2 else cols
            i = (d % d2) % half
            ang = (pos * fr[i]).astype(np.float32)
            cos[d] = np.cos(ang)
            sgn = -1.0 if (d % d2) < half else 1.0
            sin[d] = sgn * np.sin(ang)
        return cos, sin

    cA, sA = tab(64)
    cB, sB = tab(32)
    tabA = np.stack([np.tile(cA, (2, 1)), np.tile(sA, (2, 1))], axis=1)
    tabB = np.stack([np.tile(cB, (4, 1)), np.tile(sB, (4, 1))], axis=1)
    return np.ascontiguousarray(tabA), np.ascontiguousarray(tabB)


def _perm(dim):
    d2 = dim // 2
    half = d2 // 2
    p = np.arange(dim)
    for d in range(dim):
        p[d] = d + half if (d % d2) < half else d - half
    return p


def _chunk_kp(w):
    m = w.shape[1] // 128
    return np.ascontiguousarray(w.reshape(8, 128, m, 128).transpose(2, 1, 0, 3))


def _host_layout(inp):
    f = lambda k: np.asarray(inp[k], dtype=np.float32)
    w_in = f("w_in")[0]
    pA, pB = _perm(64), _perm(32)
    q = w_in[:, 0:512]
    kk = w_in[:, 512:640]
    v = w_in[:, 640:768]
    cq = w_in[:, 768:1152]
    ckv = w_in[:, 1152:1408]
    krw = w_in[:, 1408:1440]
    ga = w_in[:, 1440:2464]
    gb = w_in[:, 2464:3488]
    q_rot = q.reshape(D, 8, 64)[:, :, pA].reshape(D, 512)
    k0, k1 = kk[:, 0:64], kk[:, 64:128]
    kdup = np.concatenate([k0, k0, k1, k1], axis=1)
    kdup_rot = np.concatenate([k0[:, pA], k0[:, pA], k1[:, pA], k1[:, pA]], axis=1)
    krc = np.concatenate([krw, krw[:, pB], np.zeros((D, 64), np.float32)], axis=1)
    ext = np.concatenate([q, q_rot, kdup, kdup_rot, v, cq, ckv, krc, ga, gb], axis=1)
    assert ext.shape[1] == N_WIN_CH * 128
    shared = {}
    shared["win"] = _chunk_kp(ext)
    shared["wmod"] = _chunk_kp(f("w_mod")[0])
    shared["bmodT"] = np.ascontiguousarray(f("b_mod")[0].reshape(48, 128).T)
    w_uq = f("w_uq")[0]
    uq_rot = w_uq.reshape(384, 8, 96)[:, :, 64:96][:, :, pB].reshape(384, 256)
    shared["wuq"] = np.ascontiguousarray(np.concatenate([w_uq, uq_rot], axis=1).reshape(3, 128, 1024).transpose(1, 0, 2))
    shared["wukv"] = np.ascontiguousarray(f("w_ukv")[0].reshape(2, 128, 1024).transpose(1, 0, 2))
    shared["wpa"] = np.ascontiguousarray(f("w_proj_a")[0].reshape(4, 128, 8, 128).transpose(2, 1, 0, 3))
    shared["wpb"] = np.ascontiguousarray(f("w_proj_b")[0].reshape(4, 128, 8, 128).transpose(2, 1, 0, 3))
    shared["wout"] = np.ascontiguousarray(f("w_out")[0].reshape(8, 128, 1024).transpose(1, 0, 2))
    w_up = f("w_up")[0]
    a4 = w_up[:, :FH].reshape(8, 128, NF, 128)
    u4 = w_up[:, FH:].reshape(8, 128, NF, 128)
    shared["wup"] = np.ascontiguousarray(np.concatenate([a4, u4], axis=3).transpose(2, 1, 0, 3))
    shared["wdown"] = np.ascontiguousarray(f("w_down")[0].reshape(NF, 128, 1024).transpose(1, 0, 2))
    tabA, tabB = _rope_tables()
    shared["tabA"], shared["tabB"] = tabA, tabB
    gv = np.zeros((128, 16), np.float32)
    qg, kg = f("q_norm_a")[0], f("k_norm_a")[0]
    gv[:, 0] = np.tile(qg, 2)
    gv[:, 1] = np.tile(qg[pA], 2)
    gv[:, 2] = np.tile(kg, 2)
    gv[:, 3] = np.tile(kg[pA], 2)
    gv[:, 4:7] = f("cq_norm")[0].reshape(3, 128).T
    gv[:, 7:9] = f("ckv_norm")[0].reshape(2, 128).T
    shared["gvec"] = gv
    shared["lnrows"] = np.ascontiguousarray(np.stack([f("ln1_g")[0], f("ln1_b")[0], f("ln2_g")[0], f("ln2_b")[0]]))
    cst = np.zeros((128, 3, 128), np.float32)
    cst[:, 0, :] = np.eye(128, dtype=np.float32)
    cst[0:64, 1, 0:64] = 1.0
    cst[64:128, 1, 64:128] = 1.0
    cst[:, 2, :] = 1.0
    shared["consts"] = cst
    x, c, ctx, c_ctx = f("x"), f("c"), f("ctx"), f("c_ctx")
    in_maps = []
    for b in range(8):
        m = dict(shared)
        m["x"] = np.ascontiguousarray(x[b])
        m["ctx"] = np.ascontiguousarray(ctx[b])
        m["cT"] = np.ascontiguousarray(np.stack([c[b].reshape(8, 128).T, c_ctx.reshape(8, 128).T], axis=2))
        in_maps.append(m)
    return in_maps


_NC_CACHE = {}


def kernel(**inputs):
    in_maps = _host_layout(inputs)
    if "nc" not in _NC_CACHE:
        _NC_CACHE["nc"] = build_program()
    nc = _NC_CACHE["nc"]
    res = run_bass_kernel_spmd(nc, in_maps, core_ids=list(range(8)))
    out = np.stack([np.asarray(r["out"], dtype=np.float32) for r in res.results], axis=0)
    if DEBUG:
        kernel.last_results = res.results
    return out
```
